# Optimizing a Trainium2 kernel written in Bass

```python
import jax, jax.numpy as jnp
from jax import lax
import numpy as np

D_MODEL = 2048
BATCH = 16
SEQ = 256
DEPTH = 1
DEC_BATCH = 2
DEC_SEQ = 4096
PAST_LEN = 256

GRID_W = 64
HEAD_DIM = 128
H_RET = 8
H_ATT = 8
KV_HEADS = 2
GQA_REP = H_ATT // KV_HEADS
RET_WIDTH = H_RET * HEAD_DIM
ATT_WIDTH = H_ATT * HEAD_DIM
KV_WIDTH = KV_HEADS * HEAD_DIM
IN_SPLITS = (RET_WIDTH, 2 * RET_WIDTH, 3 * RET_WIDTH, 4 * RET_WIDTH,
             4 * RET_WIDTH + ATT_WIDTH, 4 * RET_WIDTH + ATT_WIDTH + KV_WIDTH)
IN_COLS = 4 * RET_WIDTH + ATT_WIDTH + 2 * KV_WIDTH
RET_CHUNK = 128
Q_BLOCK = 128
ROPE_BASE = 10000.0
ROPE_AXIS_DIM = HEAD_DIM // 2
N_KEYS = 128
N_EXPERTS = N_KEYS * N_KEYS
PEER_HEADS = 8
PEER_TOPK = 16
PEER_QDIM = 256
PEER_SUBDIM = PEER_QDIM // 2
PEER_TOK_BLOCK = 128
N_MOD = 6
NORM_EPS = 1e-5
DN_ALPHA = (2.0 * DEPTH) ** 0.25
DN_BETA = (8.0 * DEPTH) ** -0.25

kernel_name = "hybrid_retention_gqa_peer_dit_step"


def _layer_norm(x, g, b):
    x32 = x.astype(jnp.float32)
    mu = x32.mean(-1, keepdims=True)
    var = jnp.square(x32 - mu).mean(-1, keepdims=True)
    return ((x32 - mu) * lax.rsqrt(var + NORM_EPS) * g + b).astype(x.dtype)


def _rms_heads(x, g):
    x32 = x.astype(jnp.float32)
    return (x32 * lax.rsqrt(jnp.mean(x32 * x32, -1, keepdims=True) + NORM_EPS) * g).astype(x.dtype)


def _ada_modulation(cond, w_ada, b_ada):
    m = jax.nn.silu(cond) @ w_ada + b_ada
    return jnp.split(m[:, None, :], N_MOD, axis=-1)


def _axial_rope(x, n_tokens):
    rows = n_tokens // GRID_W
    row = jnp.repeat(jnp.arange(rows, dtype=jnp.float32), GRID_W)
    col = jnp.tile(jnp.arange(GRID_W, dtype=jnp.float32), rows)
    inv_freq = ROPE_BASE ** (-jnp.arange(0, ROPE_AXIS_DIM, 2, dtype=jnp.float32) / ROPE_AXIS_DIM)
    ang_r = row[:, None] * inv_freq
    ang_c = col[:, None] * inv_freq
    ang = jnp.concatenate([ang_r, ang_r, ang_c, ang_c], -1)[None, :, None, :]
    half = ROPE_AXIS_DIM // 2
    x1 = x[..., :half]
    x2 = x[..., half:ROPE_AXIS_DIM]
    x3 = x[..., ROPE_AXIS_DIM:ROPE_AXIS_DIM + half]
    x4 = x[..., ROPE_AXIS_DIM + half:]
    rot = jnp.concatenate([-x2, x1, -x4, x3], -1)
    return (x * jnp.cos(ang) + rot * jnp.sin(ang)).astype(x.dtype)


def _retention_dir(q, k, v, log_gamma, s0):
    b, h, L, dk = q.shape
    dv = v.shape[-1]
    nc = L // RET_CHUNK
    qc = q.reshape(b, h, nc, RET_CHUNK, dk)
    kc = k.reshape(b, h, nc, RET_CHUNK, dk)
    vc = v.reshape(b, h, nc, RET_CHUNK, dv)
    pos = jnp.arange(RET_CHUNK, dtype=jnp.float32)
    lg = log_gamma[:, None]
    diff = pos[:, None] - pos[None, :]
    dmask = jnp.where(diff >= 0, jnp.exp(lg[:, :, None] * jnp.maximum(diff, 0.0)), 0.0)
    scores = jnp.einsum('bhcid,bhcjd->bhcij', qc, kc) * dmask[None, :, None]
    intra = jnp.einsum('bhcij,bhcje->bhcie', scores, vc)
    zeta = jnp.exp(lg * (RET_CHUNK - 1.0 - pos))
    kv = jnp.einsum('bhcjd,bhcje->cbhde', kc * zeta[None, :, None, :, None], vc)
    gamma_c = jnp.exp(log_gamma * RET_CHUNK)[None, :, None, None]

    def step(s, kv_c):
        return gamma_c * s + kv_c, s

    s_final, s_prev = lax.scan(step, s0, kv)
    xi = jnp.exp(lg * (pos + 1.0))
    cross = jnp.einsum('bhcid,cbhde->bhcie', qc * xi[None, :, None, :, None], s_prev)
    return (intra + cross).reshape(b, h, L, dv), s_final


def _block_attention(q, k, v):
    b, L = q.shape[:2]
    nb = L // Q_BLOCK
    qb = q.reshape(b, nb, Q_BLOCK, KV_HEADS, GQA_REP, HEAD_DIM).transpose(1, 0, 2, 3, 4, 5)
    scale = HEAD_DIM ** -0.5

    def one_block(qblk):
        s = jnp.einsum('bqgrd,bkgd->bgrqk', qblk, k).astype(jnp.float32) * scale
        p = jax.nn.softmax(s, axis=-1).astype(v.dtype)
        return jnp.einsum('bgrqk,bkgd->bqgrd', p, v)

    o = lax.map(one_block, qb)
    return o.transpose(1, 0, 2, 3, 4, 5).reshape(b, L, ATT_WIDTH)


def _mixing(h, w_in, decay_logit, gn_g, qn_g, kn_g, w_o, ctx):
    b, L, _ = h.shape
    proj = h @ w_in
    q_r, k_r, v_r, g_r, q_a, k_a, v_a = jnp.split(proj, IN_SPLITS, axis=-1)

    def heads(t):
        return t.reshape(b, L, -1, HEAD_DIM).transpose(0, 2, 1, 3).astype(jnp.float32)

    log_gamma = jax.nn.log_sigmoid(decay_logit.astype(jnp.float32))
    if ctx is None:
        s0_f = jnp.zeros((b, H_RET, HEAD_DIM, HEAD_DIM), jnp.float32)
        s0_b = s0_f
    else:
        s0_f = ctx[0].astype(jnp.float32)
        s0_b = ctx[1].astype(jnp.float32)

    qr = heads(q_r)
    kr = heads(k_r) * HEAD_DIM ** -0.5
    vr = heads(v_r)
    o_f, s_f = _retention_dir(qr, kr, vr, log_gamma[0], s0_f)
    o_b, s_b = _retention_dir(jnp.flip(qr, 2), jnp.flip(kr, 2), jnp.flip(vr, 2), log_gamma[1], s0_b)
    o_r = o_f + jnp.flip(o_b, 2)
    mu = o_r.mean(-1, keepdims=True)
    var = jnp.square(o_r - mu).mean(-1, keepdims=True)
    o_r = ((o_r - mu) * lax.rsqrt(var + NORM_EPS)).transpose(0, 2, 1, 3).reshape(b, L, RET_WIDTH)
    o_r = (o_r * gn_g * jax.nn.silu(g_r.astype(jnp.float32))).astype(h.dtype)

    qa = _rms_heads(q_a.reshape(b, L, H_ATT, HEAD_DIM), qn_g)
    ka = _rms_heads(k_a.reshape(b, L, KV_HEADS, HEAD_DIM), kn_g)
    va = v_a.reshape(b, L, KV_HEADS, HEAD_DIM)
    if ctx is None:
        k_all, v_all = ka, va
        new_ctx = (jnp.stack([s_f, s_b], axis=1), ka, va)
    else:
        qa = _axial_rope(qa, L)
        ka = _axial_rope(ka, L)
        k_all = jnp.concatenate([ctx[2].astype(ka.dtype), ka], axis=1)
        v_all = jnp.concatenate([ctx[3].astype(va.dtype), va], axis=1)
        new_ctx = None
    o_a = _block_attention(qa.reshape(b, L, KV_HEADS, GQA_REP, HEAD_DIM), k_all, v_all)

    out = jnp.concatenate([o_r, o_a.astype(h.dtype)], axis=-1) @ w_o
    return out, new_ctx


def _peer(h, w_q, sub_keys, u_tab, v_tab):
    b, L, d = h.shape
    T = b * L
    x = h.reshape(T, d)
    q = (x @ w_q).reshape(T, PEER_HEADS, 2, PEER_SUBDIM)
    s = jnp.einsum('thpd,hpnd->thpn', q, sub_keys).astype(jnp.float32)
    top_s, top_i = lax.top_k(s, PEER_TOPK)
    cand_s = (top_s[:, :, 0, :, None] + top_s[:, :, 1, None, :]).reshape(T, PEER_HEADS, PEER_TOPK * PEER_TOPK)
    cand_i = (top_i[:, :, 0, :, None] * N_KEYS + top_i[:, :, 1, None, :]).reshape(T, PEER_HEADS, PEER_TOPK * PEER_TOPK)
    best_s, best_pos = lax.top_k(cand_s, PEER_TOPK)
    idx = jnp.take_along_axis(cand_i, best_pos, axis=-1)
    gate = jax.nn.softmax(best_s, axis=-1)
    nb = T // PEER_TOK_BLOCK

    def one_block(args):
        xt, it, gt = args
        act = jax.nn.gelu(jnp.einsum('thkd,td->thk', u_tab[it], xt).astype(jnp.float32), approximate=False)
        w = (gt * act).astype(v_tab.dtype)
        return jnp.einsum('thk,thkd->td', w, v_tab[it])

    out = lax.map(one_block, (x.reshape(nb, PEER_TOK_BLOCK, d),
                              idx.reshape(nb, PEER_TOK_BLOCK, PEER_HEADS, PEER_TOPK),
                              gate.reshape(nb, PEER_TOK_BLOCK, PEER_HEADS, PEER_TOPK)))
    return out.reshape(b, L, d)


def _layer(x, cond, p, ctx):
    (w_ada, b_ada, w_in, decay_logit, gn_g, qn_g, kn_g, w_o,
     ln1_g, ln1_b, pw_q, p_keys, p_u, p_v, ln2_g, ln2_b) = p
    sh1, sc1, g1, sh2, sc2, g2 = _ada_modulation(cond, w_ada, b_ada)
    h = x * (1.0 + sc1) + sh1
    mix, new_ctx = _mixing(h, w_in, decay_logit, gn_g, qn_g, kn_g, w_o, ctx)
    x = _layer_norm(DN_ALPHA * x + g1 * mix, ln1_g, ln1_b)
    h = x * (1.0 + sc2) + sh2
    x = _layer_norm(DN_ALPHA * x + g2 * _peer(h, pw_q, p_keys, p_u, p_v), ln2_g, ln2_b)
    return x, new_ctx


def setup_inputs(seed: int = 0) -> dict:
    key = jax.random.key(seed)
    ks = jax.random.split(key, 24)
    d = D_MODEL

    def nrm(k, shape, s):
        return jax.random.normal(k, shape, jnp.float32) * s

    col_scale = jnp.concatenate([
        jnp.ones((2 * RET_WIDTH,), jnp.float32),
        jnp.full((RET_WIDTH,), DN_BETA, jnp.float32),
        jnp.ones((RET_WIDTH + ATT_WIDTH + KV_WIDTH,), jnp.float32),
        jnp.full((KV_WIDTH,), DN_BETA, jnp.float32)])
    gam = 1.0 - 2.0 ** (-5.0 - jnp.arange(H_RET, dtype=jnp.float32))
    logit = jnp.log(gam) - jnp.log1p(-gam)
    return {
        "x_prompt": nrm(ks[0], (BATCH, SEQ, d), 1.0),
        "x_sample": nrm(ks[1], (DEC_BATCH, DEC_SEQ, d), 1.0),
        "state_ret": nrm(ks[2], (DEC_BATCH, DEPTH, 2, H_RET, HEAD_DIM, HEAD_DIM), 0.5),
        "cache_k": nrm(ks[3], (DEC_BATCH, DEPTH, PAST_LEN, KV_HEADS, HEAD_DIM), 1.0),
        "cache_v": nrm(ks[4], (DEC_BATCH, DEPTH, PAST_LEN, KV_HEADS, HEAD_DIM), 1.0),
        "c": nrm(ks[5], (DEC_BATCH, d), 1.0),
        "c_ctx": nrm(ks[6], (d,), 1.0),
        "w_ada": nrm(ks[7], (DEPTH, d, N_MOD * d), 0.5 * d ** -0.5),
        "b_ada": nrm(ks[8], (DEPTH, N_MOD * d), 0.02),
        "w_in": nrm(ks[9], (DEPTH, d, IN_COLS), d ** -0.5) * col_scale,
        "ret_decay_logit": jnp.broadcast_to(logit, (DEPTH, 2, H_RET)) + nrm(ks[10], (DEPTH, 2, H_RET), 0.1),
        "ret_gn_g": 1.0 + nrm(ks[11], (DEPTH, RET_WIDTH), 0.02),
        "q_norm_g": 1.0 + nrm(ks[12], (DEPTH, HEAD_DIM), 0.02),
        "k_norm_g": 1.0 + nrm(ks[13], (DEPTH, HEAD_DIM), 0.02),
        "w_o": nrm(ks[14], (DEPTH, d, d), DN_BETA * d ** -0.5),
        "ln1_g": 1.0 + nrm(ks[15], (DEPTH, d), 0.02),
        "ln1_b": nrm(ks[16], (DEPTH, d), 0.02),
        "peer_w_q": nrm(ks[17], (DEPTH, d, PEER_HEADS * PEER_QDIM), d ** -0.5),
        "peer_sub_keys": nrm(ks[18], (DEPTH, PEER_HEADS, 2, N_KEYS, PEER_SUBDIM), PEER_SUBDIM ** -0.5),
        "peer_u": nrm(ks[19], (DEPTH, N_EXPERTS, d), d ** -0.5),
        "peer_v": nrm(ks[20], (DEPTH, N_EXPERTS, d), DN_BETA * PEER_HEADS ** -0.5),
        "ln2_g": 1.0 + nrm(ks[21], (DEPTH, d), 0.02),
        "ln2_b": nrm(ks[22], (DEPTH, d), 0.02),
    }


def reference(x_prompt, x_sample, state_ret, cache_k, cache_v, c, c_ctx, w_ada, b_ada, w_in,
              ret_decay_logit, ret_gn_g, q_norm_g, k_norm_g, w_o, ln1_g, ln1_b,
              peer_w_q, peer_sub_keys, peer_u, peer_v, ln2_g, ln2_b):
    y_p = x_prompt
    y_s = x_sample
    cond_ctx = c_ctx[None, :]
    new_ret, new_k, new_v = [], [], []
    for i in range(DEPTH):
        p = (w_ada[i], b_ada[i], w_in[i], ret_decay_logit[i], ret_gn_g[i], q_norm_g[i], k_norm_g[i], w_o[i],
             ln1_g[i], ln1_b[i], peer_w_q[i], peer_sub_keys[i], peer_u[i], peer_v[i], ln2_g[i], ln2_b[i])
        y_p, (s_ret, k_c, v_c) = _layer(y_p, cond_ctx, p, None)
        new_ret.append(s_ret)
        new_k.append(k_c)
        new_v.append(v_c)
        ctx = (state_ret[:, i, 0], state_ret[:, i, 1], cache_k[:, i], cache_v[:, i])
        y_s, _ = _layer(y_s, c, p, ctx)
    new_state_ret = jnp.stack(new_ret, axis=1)
    new_cache_k = jnp.stack(new_k, axis=1)
    new_cache_v = jnp.stack(new_v, axis=1)
    return (y_p, y_s, new_state_ret, new_cache_k, new_cache_v)
```

```python
import contextlib
import numpy as np
import concourse.bass as bass
import concourse.mybir as mybir
from concourse.bass_utils import run_bass_kernel_spmd

F32 = mybir.dt.float32
BF16 = mybir.dt.bfloat16
F32R = mybir.dt.float32r
ALU = mybir.AluOpType
AF = mybir.ActivationFunctionType
AX = mybir.AxisListType

D = 2048
NT = 12
NO = 24
EPS = 1e-5
ALPHA = 2.0 ** 0.25
ENGS = ("pe", "act", "dve", "pool", "sp")
N_DMA_SEMS = 24


class Sched:
    def __init__(self, nc):
        self.nc = nc
        self.items = {e: [] for e in ENGS}
        self.count = {}
        self.known = {e: {} for e in ENGS}
        self.last_w = {}
        self.readers = {}
        self.dma_i = 0
        self.nops = 0

    def _need(self, eng, tok):
        sk, v = tok
        if self.known[eng].get(sk, 0) >= v:
            return
        if sk == eng and eng == "pe":
            return
        self.items[eng].append(("w", sk, v))
        self.known[eng][sk] = v

    def op(self, eng, fn, reads=(), writes=(), dma=False):
        deps = []
        for r in reads:
            if r in self.last_w:
                deps.append(self.last_w[r])
        for w in writes:
            if w in self.last_w and self.last_w[w][0] != eng:
                deps.append(self.last_w[w])
            deps.extend(t for t in self.readers.get(w, ()) if t[0] != eng)
        best = {}
        for sk, v in deps:
            if best.get(sk, 0) < v:
                best[sk] = v
        for sk, v in best.items():
            self._need(eng, (sk, v))
        if dma:
            sk = ("dma", self.dma_i % N_DMA_SEMS)
            self.dma_i += 1
            prev = self.count.get(sk, 0)
            if prev:
                self._need(eng, (sk, prev))
            val, inc = prev + 16, 16
        else:
            sk = eng
            val, inc = self.count.get(sk, 0) + 1, 1
        self.count[sk] = val
        tok = (sk, val)
        self.items[eng].append(("o", fn, sk, inc))
        for r in reads:
            self.readers.setdefault(r, []).append(tok)
        for w in writes:
            self.last_w[w] = tok
            self.readers[w] = []
        self.nops += 1
        return tok

    def barrier(self):
        for e in ENGS:
            for sk, v in list(self.count.items()):
                self._need(e, (sk, v))
        self.last_w = {}
        self.readers = {}

    def emit(self):
        nc = self.nc
        with contextlib.ExitStack() as es:
            sems = {}
            for sk in self.count:
                name = "s_" + (sk if isinstance(sk, str) else "dma%d" % sk[1])
                sems[sk] = es.enter_context(nc.semaphore(name))
            block = es.enter_context(nc.Block())
            engmap = {"pe": block.tensor, "act": block.scalar, "dve": block.vector,
                      "pool": block.gpsimd, "sp": block.sync}

            def mk(items):
                def body(eng):
                    for it in items:
                        if it[0] == "w":
                            eng.wait_ge(sems[it[1]], it[2])
                        else:
                            it[1](eng).then_inc(sems[it[2]], it[3])
                return body

            for e in ENGS:
                if self.items[e]:
                    engmap[e](mk(self.items[e]))


def build(nc, stop_after=99, dbg=()):
    es = contextlib.ExitStack()
    S = Sched(nc)
    dbg_out = {}

    def din(name, shape, dt=F32):
        return nc.dram_tensor(name, list(shape), dt, kind="ExternalInput").ap()

    def dscr(name, shape, dt=BF16):
        return nc.dram_tensor(name, list(shape), dt, kind="Internal").ap()

    def dout(name, shape, dt=F32):
        return nc.dram_tensor(name, list(shape), dt, kind="ExternalOutput").ap()

    xown = din("xown", [NT * 128, D]); xoth = din("xoth", [NO * 128, D])
    cond_d = din("cond", [2, D]); s0_d = din("s0", [16, 128, 128])
    ck_d = din("ck", [256, 256]); cv_d = din("cv", [256, 256])
    w_ada = din("w_ada", [D, 6 * D]); b_ada = din("b_ada", [1, 6 * D]); w_in = din("w_in", [D, 5632])
    decay_d = din("decay", [1, 16]); gng_d = din("gn_g", [1, 1024]); qng_d = din("qn_g", [1, 128]); kng_d = din("kn_g", [1, 128])
    w_o = din("w_o", [D, D]); ln1g_d = din("ln1_g", [1, D]); ln1b_d = din("ln1_b", [1, D])
    w_q = din("w_q", [D, D]); keysT_d = din("keysT", [16, 128, 128]); UT_d = din("UT", [128, 128, D]); V_d = din("V", [16384, D])
    ln2g_d = din("ln2_g", [1, D]); ln2b_d = din("ln2_b", [1, D])
    ident_d = din("ident", [128, 128]); pert_d = din("pert", [1, 2048])
    cos_o = din("cos_own", [1024, 128]); sin_o = din("sin_own", [1024, 128])
    cos_x = din("cos_oth", [3072, 128]); sin_x = din("sin_oth", [3072, 128])
    A1_d = din("A1", [128, 128]); A2_d = din("A2", [128, 128]); U1_d = din("U1", [128, 128]); L1_d = din("L1", [128, 128])
    zexp_d = din("zexp", [128, 2]); xirow_d = din("xirow", [2, 128])
    oexp_d = din("oexp", [128, 2, NO]); omsk_d = din("omsk", [128, 2, NO]); s0c_d = din("s0c", [1, 2])
    y_d = dout("y", [NT * 128, D]); nst_d = dout("nst", [2, 2, 8, 128, 128])
    nk_d = dout("nk", [2, 256, 256]); nv_d = dout("nv", [2, 256, 256])
    MODD = dscr("MODD", [2, 6 * D], F32)
    QRT = dscr("QRT", [8, 128, NT * 128]); KRT = dscr("KRT", [8, 128, NT * 128])
    KR = dscr("KR", [NT + NO, 128, 1024]); VR = dscr("VR", [NT + NO, 128, 1024]); GR = dscr("GR", [NT, 128, 1024])
    QAT = dscr("QAT", [8, 128, NT * 128]); KAT = dscr("KAT", [2, 128, (NT + NO) * 128]); VA = dscr("VA", [NT + NO, 128, 256])
    OT = dscr("OT", [NT, 128, 16, 128]); X1 = dscr("X1", [NT, 128, D], F32); H2T = dscr("H2T", [128, 16, NT * 128])
    SC = dscr("SC", [NT, 128, D], F32); WD = dscr("WD", [2, 128, 128, 768])

    ARENA = 105000
    arena_t = es.enter_context(nc.sbuf_tensor("arena", [128, ARENA], BF16))
    psum_t = es.enter_context(nc.psum_tensor("psum", [128, 4096], F32))
    st = {"off": 0, "top": ARENA}

    def sb(shape, dt=F32, top=False):
        n = int(np.prod(shape))
        nb = n * 2 if dt == F32 else n
        nb = (nb + 15) // 16 * 16
        if top:
            st["top"] -= nb
            assert st["top"] >= st["off"], ("SBUF arena overflow (top)", st["off"], st["top"])
            ap = arena_t[:, st["top"]:st["top"] + nb]
        else:
            assert st["off"] + nb <= st["top"], ("SBUF arena overflow", st["off"], nb, st["top"])
            ap = arena_t[:, st["off"]:st["off"] + nb]
            st["off"] += nb
        if dt == F32:
            ap = ap.bitcast(F32)[:, 0:n]
        else:
            ap = ap[:, 0:n]
        if len(shape) == 2:
            ap = ap.rearrange("p (a b) -> p a b", a=shape[0], b=shape[1])
        elif len(shape) == 3:
            ap = ap.rearrange("p (a b c) -> p a b c", a=shape[0], b=shape[1], c=shape[2])
        elif len(shape) == 4:
            ap = ap.rearrange("p (a b c d) -> p a b c d", a=shape[0], b=shape[1], c=shape[2], d=shape[3])
        return ap

    def ps(bank, nbanks=1, dt=F32):
        ap = psum_t[:, bank * 512:(bank + nbanks) * 512]
        if dt == BF16:
            ap = ap.bitcast(BF16)
        return ap

    def dma(eng, out, in_, reads=(), writes=(), slow=False):
        if slow:
            fn = lambda e: e.dma_start(out=out, in_=in_, allow_slow_non_contiguous=True)
        else:
            fn = lambda e: e.dma_start(out=out, in_=in_)
        S.op(eng, fn, reads, writes, dma=True)

    def load(out, in_, w, reads=(), slow=False):
        dma("sp", out, in_, reads, [w] if isinstance(w, str) else w, slow)

    def store(out, in_, r, writes=(), slow=False):
        dma("sp", out, in_, [r] if isinstance(r, str) else r, writes, slow)

    def tt(eng, out, in0, in1, op, reads, writes):
        S.op(eng, lambda e: e.tensor_tensor(out=out, in0=in0, in1=in1, op=op), reads, writes)

    def ts(eng, out, in0, s1, op0, reads, writes, s2=None, op1=None):
        if op1 is None:
            S.op(eng, lambda e: e.tensor_single_scalar(out=out, in_=in0, scalar=s1, op=op0), reads, writes)
        else:
            S.op(eng, lambda e: e.tensor_scalar(out=out, in0=in0, scalar1=s1, scalar2=s2, op0=op0, op1=op1), reads, writes)

    def stt(eng, out, in0, scalar, in1, op0, op1, reads, writes):
        S.op(eng, lambda e: e.scalar_tensor_tensor(out=out, in0=in0, scalar=scalar, in1=in1, op0=op0, op1=op1), reads, writes)

    def act(out, in_, func, reads, writes, bias=None, scale=None, accum=None):
        kw = {}
        if bias is not None:
            kw["bias"] = bias
        if scale is not None:
            kw["scale"] = scale
        if accum is not None:
            kw["accum_out"] = accum
        S.op("act", lambda e: e.activation(out=out, in_=in_, func=func, **kw), reads, writes)

    def cp(eng, out, in_, reads, writes):
        if eng == "act":
            S.op("act", lambda e: e.copy(out=out, in_=in_), reads, writes)
        else:
            S.op(eng, lambda e: e.tensor_copy(out=out, in_=in_), reads, writes)

    def mms(out, pairs, reads, writes):
        def fn(e):
            n = len(pairs)
            for i, (l, r) in enumerate(pairs):
                ins = e.matmul(out, lhsT=l, rhs=r, start=(i == 0), stop=(i == n - 1))
            return ins
        S.op("pe", fn, reads, writes)

    def mmx(ops, reads, writes):
        def fn(e):
            for o, l, r, a, b in ops:
                ins = e.matmul(o, lhsT=l, rhs=r, start=a, stop=b)
            return ins
        S.op("pe", fn, reads, writes)

    def transposes(items, ident, reads, writes):
        def fn(e):
            for o, i in items:
                ins = e.transpose(out=o, in_=i, identity=ident)
            return ins
        S.op("pe", fn, reads, writes)

    def bc(ap, axis, shape):
        return ap.unsqueeze(axis).to_broadcast(list(shape))

    ident_f = sb([128]); ident_b = sb([128], BF16); ones_b = sb([128], BF16)
    modc = sb([2, 4, 16])
    epsc = sb([1])
    P_LATE = st["off"]
    lg = sb([16])
    gC = sb([16]); cs0 = sb([16]); zz = sb([16])
    xibc = sb([16, 128]); DT = sb([8, 128])
    wo = sb([2, NO, 8])
    P_MARK = st["off"]

    load(ident_f, ident_d, "ident_f")
    cp("dve", ident_b, ident_f, ["ident_f"], ["ident_b"])
    S.op("dve", lambda e: e.memset(ones_b, 1.0), [], ["ones_b"])
    S.op("dve", lambda e: e.memset(epsc, EPS), [], ["epsc"])

    def phase0():
        condT = sb([16, 2]); scT = sb([16, 2]); bada = sb([6 * D])
        wa = [sb([16, 512]) for _ in range(2)]
        mrow = [sb([512]) for _ in range(2)]
        for k_ in range(2):
            load(condT[:, :, k_], cond_d[k_, :].rearrange("(c p) -> p c", p=128), "condT", slow=True)
        act(scT, condT, AF.Silu, ["condT"], ["scT"])
        load(bada[0:2, :], b_ada.partition_broadcast(2), "bada")
        pm = [ps(0), ps(1)]
        load(wa[0], w_ada[:, 0:512].rearrange("(c p) n -> p c n", p=128), "wa0")
        for cb in range(24):
            k = cb % 2
            if cb + 1 < 24:
                load(wa[1 - k], w_ada[:, (cb + 1) * 512:(cb + 2) * 512].rearrange("(c p) n -> p c n", p=128), "wa%d" % (1 - k))
            mms(pm[k][0:2, :], [(scT[:, dk, :], wa[k][:, dk, :]) for dk in range(16)], ["scT", "wa%d" % k], ["pm%d" % k])
            tt("dve", mrow[k][0:2, :], pm[k][0:2, :], bada[0:2, cb * 512:(cb + 1) * 512], ALU.add, ["pm%d" % k, "bada"], ["mrow%d" % k])
            store(MODD[:, cb * 512:(cb + 1) * 512], mrow[k][0:2, :], "mrow%d" % k, ["MODD"])
        for wi, off in enumerate((1 * D, 0, 4 * D, 3 * D)):
            for k_ in range(2):
                load(modc[:, k_, wi, :], MODD[k_, off:off + D].rearrange("(c p) -> p c", p=128), "modc%d" % wi, reads=["MODD"], slow=True)
        for wi in (0, 2):
            ts("dve", modc[:, :, wi, :], modc[:, :, wi, :], 1.0, ALU.add, ["modc%d" % wi], ["modc%d" % wi])
        dl = sb([16]); y_ = sb([16]); u_ = sb([16]); lnp = sb([16]); msk = sb([16])
        zexp = sb([2]); xirow = sb([2, 128]); s0c = sb([2]); A1 = sb([128]); A2 = sb([128]); U1 = sb([128]); L1 = sb([128])
        oexp = sb([2, NO]); omsk = sb([2, NO]); e1 = sb([128]); e2 = sb([128]); wtmp = sb([NO, 8])
        load(dl, decay_d.partition_broadcast(128), "dl")
        load(zexp, zexp_d, "zexp"); load(s0c, s0c_d.partition_broadcast(128), "s0c")
        for r in range(2):
            load(xirow[:, r, :], xirow_d[r:r + 1, :].partition_broadcast(128), "xirow")
        load(A1, A1_d, "A1"); load(A2, A2_d, "A2"); load(U1, U1_d, "U1"); load(L1, L1_d, "L1")
        load(oexp, oexp_d, "oexp"); load(omsk, omsk_d, "omsk")
        act(y_, dl, AF.Exp, ["dl"], ["y_"], scale=-1.0)
        ts("dve", u_, y_, -1.0 / 8, ALU.mult, ["y_"], ["u_"])
        for k in range(7, 0, -1):
            stt("dve", u_, u_, ((-1.0) ** (k + 1)) / k, y_, ALU.add, ALU.mult, ["u_", "y_"], ["u_"])
        act(lnp, y_, AF.Ln, ["y_"], ["lnp"], bias=1.0)
        ts("dve", msk, y_, 0.3, ALU.is_le, ["y_"], ["msk"])
        tt("dve", u_, u_, lnp, ALU.subtract, ["u_", "lnp"], ["u_"])
        tt("dve", u_, u_, msk, ALU.mult, ["u_", "msk"], ["u_"])
        tt("dve", u_, u_, lnp, ALU.add, ["u_", "lnp"], ["u_"])
        ts("dve", lg, u_, -1.0, ALU.mult, ["u_"], ["lg"])
        act(gC, lg, AF.Exp, ["lg"], ["gC"], scale=128.0)
        for d_ in range(2):
            act(cs0[:, d_ * 8:(d_ + 1) * 8], lg[:, d_ * 8:(d_ + 1) * 8], AF.Exp, ["lg", "s0c"], ["cs0"], scale=s0c[:, d_:d_ + 1])
            act(zz[:, d_ * 8:(d_ + 1) * 8], lg[:, d_ * 8:(d_ + 1) * 8], AF.Exp, ["lg", "zexp"], ["zz"], scale=zexp[:, d_:d_ + 1])
            for h in range(8):
                act(xibc[:, d_ * 8 + h, :], xirow[:, d_, :], AF.Exp, ["lg", "xirow"], ["xibc"], scale=lg[:, d_ * 8 + h:d_ * 8 + h + 1])
            tt("dve", wtmp, bc(oexp[:, d_, :], 2, [128, NO, 8]), bc(lg[:, d_ * 8:(d_ + 1) * 8], 1, [128, NO, 8]), ALU.mult, ["oexp", "lg"], ["wtmp"])
            act(wtmp, wtmp, AF.Exp, ["wtmp"], ["wtmp"])
            tt("dve", wo[:, d_, :, :], wtmp, bc(omsk[:, d_, :], 2, [128, NO, 8]), ALU.mult, ["wtmp", "omsk"], ["wo"])
        for h in range(8):
            act(e1, A1, AF.Exp, ["A1", "lg"], ["e1"], scale=lg[:, h:h + 1])
            tt("dve", e1, e1, U1, ALU.mult, ["e1", "U1"], ["e1"])
            act(e2, A2, AF.Exp, ["A2", "lg"], ["e2"], scale=lg[:, 8 + h:9 + h])
            tt("dve", e2, e2, L1, ALU.mult, ["e2", "L1"], ["e2"])
            tt("dve", DT[:, h, :], e1, e2, ALU.add, ["e1", "e2"], ["DT"])

    phase0()
    S.barrier()
    st["off"] = P_MARK
    if "mod" in dbg:
        dbg_out["d_modc"] = dout("d_modc", [128, 128]); dbg_out["d_ret"] = dout("d_ret", [128, 64])
        store(dbg_out["d_modc"], modc.rearrange("p a b c -> p (a b c)"), [])
        store(dbg_out["d_ret"][:, 0:16], lg, []); store(dbg_out["d_ret"][:, 16:32], gC, [])
        store(dbg_out["d_ret"][:, 32:48], cs0, []); store(dbg_out["d_ret"][:, 48:64], zz, [])

    def rope(eng, dst, src, cs, nh, rk, wk):
        tmp = rope_tmp[:, 0:nh, :]
        tv = tmp.rearrange("p h (a s d) -> p h a s d", a=2, s=2)
        sv = src.rearrange("p h (a s d) -> p h a s d", a=2, s=2)
        snv = cs[:, 1, :].rearrange("p (a s d) -> p a s d", a=2, s=2)
        for s_ in range(2):
            tt(eng, tv[:, :, :, s_, :], sv[:, :, :, 1 - s_, :], bc(snv[:, :, s_, :], 1, [128, nh, 2, 32]), ALU.mult, [rk, "cs"], ["rope_tmp"])
        tt(eng, src, src, bc(cs[:, 0, :], 1, [128, nh, 128]), ALU.mult, [rk, "cs", "rope_tmp"], [rk])
        tt(eng, dst, src, tmp, ALU.add, [rk, "rope_tmp"], [wk])

    if stop_after >= 1:
        hT = sb([NT, 16, 128], BF16)
        xt = [sb([D]) for _ in range(2)]
        wst = [sb([16, 512]) for _ in range(2)]
        wb = [sb([16, 512], BF16) for _ in range(2)]
        ebf = [sb([512], BF16) for _ in range(2)]
        trs = [sb([4, 128], BF16) for _ in range(2)]
        nrm = [sb([4, 128]) for _ in range(2)]
        nbf = [sb([4, 128], BF16) for _ in range(2)]
        vf = [sb([256]) for _ in range(2)]
        cs_t = [sb([2, 128]) for _ in range(2)]
        rope_tmp = sb([4, 128])
        ssq = sb([8]); rstd = sb([8]); junk = sb([128])
        qg = sb([128]); kg = sb([128])
        load(qg, qng_d.partition_broadcast(128), "qg"); load(kg, kng_d.partition_broadcast(128), "kg")
        pT = ps(0, 4).rearrange("p (a b) -> p a b", a=16)
        pp = [ps(4), ps(5)]
        ptr = [ps(6, 1, BF16)[:, 0:512].rearrange("p (a b) -> p a b", a=4), ps(7, 1, BF16)[:, 0:512].rearrange("p (a b) -> p a b", a=4)]
        cnt = {"e": 0, "x": 0}

        def rmsnorm(src_ps, nh, gain, dst, rk, wk):
            for hh in range(nh):
                act(junk, src_ps[:, hh * 128:(hh + 1) * 128], AF.Square, [rk, "junk"], ["junk", "ssq"], accum=ssq[:, hh:hh + 1])
            act(rstd[:, 0:nh], ssq[:, 0:nh], AF.Sqrt, ["ssq", "epsc"], ["rstd"], bias=epsc[:, 0:1], scale=1.0 / 128)
            S.op("dve", lambda e: e.reciprocal(out=rstd[:, 0:nh], in_=rstd[:, 0:nh]), ["rstd"], ["rstd"])
            for hh in range(nh):
                stt("dve", dst[:, hh, :], src_ps[:, hh * 128:(hh + 1) * 128], rstd[:, hh:hh + 1], gain, ALU.mult, ALU.mult,
                    [rk, "rstd", "qg", "kg"], [wk])

        groups = [(list(range(0, 12)), list(range(11))), (list(range(12, 24)), [2, 3, 4, 5, 10]), (list(range(24, 36)), [2, 3, 4, 5, 10])]
        for tiles, blocks in groups:
            for li, T in enumerate(tiles):
                k = cnt["x"] % 2; cnt["x"] += 1
                src = xown[T * 128:(T + 1) * 128, :] if T < NT else xoth[(T - NT) * 128:(T - NT + 1) * 128, :]
                load(xt[k], src, "xt%d" % k)
                transposes([(pT[:, dk, :], xt[k][:, dk * 128:(dk + 1) * 128]) for dk in range(16)], ident_f, ["xt%d" % k, "ident_f"], ["pT"])
                c_ = 0 if T < 4 else 1
                for dk in range(16):
                    act(hT[:, li, dk, :], pT[:, dk, :], AF.Identity, ["pT", "modc0", "modc1"], ["hT%d" % li],
                        bias=modc[:, c_, 1, dk:dk + 1], scale=modc[:, c_, 0, dk:dk + 1])
            def wload(bi):
                cb = blocks[bi]
                load(wst[bi % 2], w_in[:, cb * 512:(cb + 1) * 512].rearrange("(c p) n -> p c n", p=128), "wst%d" % (bi % 2))
            wload(0)
            for bi, cb in enumerate(blocks):
                kb = bi % 2
                cp("dve", wb[kb][:, 0:8, :], wst[kb][:, 0:8, :], ["wst%d" % kb], ["wb%da" % kb])
                cp("act", wb[kb][:, 8:16, :], wst[kb][:, 8:16, :], ["wst%d" % kb], ["wb%db" % kb])
                if bi + 1 < len(blocks):
                    wload(bi + 1)
                pending = []

                def tr_store(k, items, srckey, nh, dst):
                    def f_():
                        transposes([(ptr[k][:, hh, :], it) for hh, it in enumerate(items)], ident_b, [srckey, "ident_b"], ["ptr%d" % k])
                        cp("dve", trs[k][:, 0:nh, :], ptr[k][:, 0:nh, :], ["ptr%d" % k], ["trs%d" % k])
                        store(dst, trs[k][:, 0:nh, :], "trs%d" % k, ["scr"])
                    pending.append(f_)

                for li, T in enumerate(tiles):
                    k = cnt["e"] % 2; cnt["e"] += 1
                    P = pp[k]; pk = "pp%d" % k
                    own = T < NT
                    mms(P, [(hT[:, li, dk, :], wb[kb][:, dk, :]) for dk in range(16)], ["hT%d" % li, "wb%da" % kb, "wb%db" % kb], [pk])
                    for f_ in pending:
                        f_()
                    del pending[:]
                    E = ebf[k]; ek = "ebf%d" % k
                    if cb in (0, 1):
                        cp("act", E, P, [pk], [ek])
                        tr_store(k, [E[:, hh * 128:(hh + 1) * 128] for hh in range(4)], ek, 4,
                                 QRT[cb * 4:cb * 4 + 4, :, T * 128:(T + 1) * 128].rearrange("h p t -> p h t"))
                    elif cb in (2, 3):
                        act(E, P, AF.Copy, [pk], [ek], scale=128.0 ** -0.5)
                        store(KR[T, :, (cb - 2) * 512:(cb - 1) * 512], E, ek, ["KR"])
                        if own:
                            tr_store(k, [E[:, hh * 128:(hh + 1) * 128] for hh in range(4)], ek, 4,
                                     KRT[(cb - 2) * 4:(cb - 2) * 4 + 4, :, T * 128:(T + 1) * 128].rearrange("h p t -> p h t"))
                    elif cb in (4, 5):
                        cp("act", E, P, [pk], [ek])
                        store(VR[T, :, (cb - 4) * 512:(cb - 3) * 512], E, ek, ["VR"])
                    elif cb in (6, 7):
                        act(E, P, AF.Silu, [pk], [ek])
                        store(GR[T, :, (cb - 6) * 512:(cb - 5) * 512], E, ek, ["GR"])
                    elif cb in (8, 9):
                        rmsnorm(P, 4, qg, nrm[k], pk, "nrm%d" % k)
                        if T >= 4:
                            load(cs_t[k][:, 0, :], cos_o[(T - 4) * 128:(T - 3) * 128, :], "cs")
                            load(cs_t[k][:, 1, :], sin_o[(T - 4) * 128:(T - 3) * 128, :], "cs")
                            rope("dve", nbf[k], nrm[k], cs_t[k], 4, "nrm%d" % k, "nbf%d" % k)
                        else:
                            cp("dve", nbf[k], nrm[k], ["nrm%d" % k], ["nbf%d" % k])
                        tr_store(k, [nbf[k][:, hh, :] for hh in range(4)], "nbf%d" % k, 4,
                                 QAT[(cb - 8) * 4:(cb - 8) * 4 + 4, :, T * 128:(T + 1) * 128].rearrange("h p t -> p h t"))
                    else:
                        rmsnorm(P, 2, kg, nrm[k], pk, "nrm%d" % k)
                        if T < 4:
                            sq, r0 = T // 2, (T % 2) * 128
                            store(nk_d[sq, r0:r0 + 128, :], nrm[k][:, 0:2, :].rearrange("p h d -> p (h d)"), "nrm%d" % k)
                            cp("act", vf[k], P[:, 256:512], [pk], ["vf%d" % k])
                            store(nv_d[sq, r0:r0 + 128, :], vf[k], "vf%d" % k)
                            cp("dve", nbf[k][:, 0:2, :], nrm[k][:, 0:2, :], ["nrm%d" % k], ["nbf%d" % k])
                        else:
                            ctab, stab, r0 = (cos_o, sin_o, (T - 4) * 128) if own else (cos_x, sin_x, (T - NT) * 128)
                            load(cs_t[k][:, 0, :], ctab[r0:r0 + 128, :], "cs")
                            load(cs_t[k][:, 1, :], stab[r0:r0 + 128, :], "cs")
                            rope("dve", nbf[k][:, 0:2, :], nrm[k][:, 0:2, :], cs_t[k], 2, "nrm%d" % k, "nbf%d" % k)
                        tr_store(k, [nbf[k][:, hh, :] for hh in range(2)], "nbf%d" % k, 2,
                                 KAT[:, :, T * 128:(T + 1) * 128].rearrange("h p t -> p h t"))
                        cp("act", E[:, 0:256], P[:, 256:512], [pk], [ek])
                        store(VA[T], E[:, 0:256], ek, ["VA"])
                for f_ in pending:
                    f_()
                del pending[:]
        S.barrier()
        st["off"] = P_MARK


    if stop_after >= 2:
        OTs = sb([NT, 16, 128], BF16)
        P2_MARK = st["off"]
        KRo = sb([NO, 1024], BF16); VRo = sb([NO, 1024], BF16)
        kw = [sb([NO, 128], BF16) for _ in range(2)]
        s0t = sb([16, 128]); Sent = sb([16, 128])
        for t in range(NO):
            load(KRo[:, t, :], KR[NT + t], "KRo")
            load(VRo[:, t, :], VR[NT + t], "VRo")
        load(s0t, s0_d.rearrange("i p v -> p i v"), "s0t")
        accS = ps(0, 4).rearrange("p (a b) -> p a b", a=16)
        for idx in range(16):
            d_, h = idx // 8, idx % 8
            k = idx % 2
            tt("dve", kw[k], KRo[:, :, h * 128:(h + 1) * 128], bc(wo[:, d_, :, h], 2, [128, NO, 128]), ALU.mult, ["KRo"], ["kw%d" % k])
            mms(accS[:, idx, :], [(kw[k][:, t, :], VRo[:, t, h * 128:(h + 1) * 128]) for t in range(NO)], ["kw%d" % k, "VRo"], ["accS%d" % idx])
            stt("dve", Sent[:, idx, :], s0t[:, idx, :], cs0[:, idx:idx + 1], accS[:, idx, :], ALU.mult, ALU.add, ["s0t", "accS%d" % idx], ["Sent"])
        S.barrier()
        if "sent" in dbg:
            dbg_out["d_sent"] = dout("d_sent", [128, 16, 128])
            store(dbg_out["d_sent"], Sent, [])
            S.barrier()
        st["off"] = P2_MARK
        Sent2 = sb([16, 128])
        cp("dve", Sent2, Sent, [], ["Sent2"])
        S.barrier()
        Sent = Sent2
        QT = [sb([1024], BF16) for _ in range(2)]; KT = [sb([1024], BF16) for _ in range(2)]
        Kc = [sb([8, 128], BF16) for _ in range(2)]; Vc = [sb([8, 128], BF16) for _ in range(2)]; Gc = [sb([8, 128], BF16) for _ in range(2)]
        kzf = sb([8, 128], BF16); kzb = sb([8, 128], BF16)
        Sf = sb([9, 128]); Sb_ = sb([9, 128]); Sfb = sb([8, 128], BF16); Sbb = sb([8, 128], BF16)
        Qxf = sb([1024], BF16); Qxb = sb([1024], BF16)
        PT = [sb([128], BF16) for _ in range(4)]; on_ = [sb([128]) for _ in range(4)]; og = [sb([128], BF16) for _ in range(4)]
        bst = [sb([6]) for _ in range(4)]; mv = [sb([2]) for _ in range(4)]; rs_ = [sb([1]) for _ in range(4)]
        gng = sb([1024])
        load(gng, gng_d.partition_broadcast(128), "gng")
        pkvf = ps(0, 2).rearrange("p (a b) -> p a b", a=8); pkvb = ps(2, 2).rearrange("p (a b) -> p a b", a=8)
        pS = [ps(4)[:, q * 128:(q + 1) * 128] for q in range(4)]
        po = [ps(5)[:, q * 128:(q + 1) * 128] for q in range(4)]
        ptr2 = [ps(6, 1, BF16)[:, q * 128:(q + 1) * 128] for q in range(4)]
        Sfb2 = [Sfb, sb([8, 128], BF16)]; Sbb2 = [Sbb, sb([8, 128], BF16)]
        Qxf2 = [Qxf, sb([1024], BF16)]; Qxb2 = [Qxb, sb([1024], BF16)]
        iters = [(si, t0, nt, h) for si, (t0, nt) in enumerate(((0, 2), (2, 2), (4, 8))) for h in range(8)]
        ccs = {"n": 0}

        def prologue(it):
            si, t0, nt, h = iters[it]
            k = it % 2
            L = nt * 128; c0 = t0 * 128
            load(QT[k][:, 0:L], QRT[h, :, c0:c0 + L], "QT%d" % k)
            load(KT[k][:, 0:L], KRT[h, :, c0:c0 + L], "KT%d" % k)
            load(Kc[k][:, 0:nt, :], KR[t0:t0 + nt, :, h * 128:(h + 1) * 128].rearrange("t p c -> p t c"), "Kc%d" % k)
            load(Vc[k][:, 0:nt, :], VR[t0:t0 + nt, :, h * 128:(h + 1) * 128].rearrange("t p c -> p t c"), "Vc%d" % k)
            load(Gc[k][:, 0:nt, :], GR[t0:t0 + nt, :, h * 128:(h + 1) * 128].rearrange("t p c -> p t c"), "Gc%d" % k)
            ts("dve", kzf[:, 0:nt, :], Kc[k][:, 0:nt, :], zz[:, h:h + 1], ALU.mult, ["Kc%d" % k], ["kzf"])
            ts("dve", kzb[:, 0:nt, :], Kc[k][:, 0:nt, :], zz[:, 8 + h:9 + h], ALU.mult, ["Kc%d" % k], ["kzb"])
            mmx([(pkvf[:, c, :], kzf[:, c, :], Vc[k][:, c, :], True, True) for c in range(nt)] +
                [(pkvb[:, c, :], kzb[:, c, :], Vc[k][:, c, :], True, True) for c in range(nt)], ["kzf", "kzb", "Vc%d" % k], ["pkv"])
            if si == 2:
                cp("dve", Sf[:, 0, :], Sent[:, h, :], ["Sent2"], ["Sf"])
                cp("dve", Sb_[:, nt, :], Sent[:, 8 + h, :], ["Sent2"], ["Sb"])
            else:
                S.op("dve", lambda e: e.memset(Sf[:, 0, :], 0.0), ["Sf"], ["Sf"])
                S.op("dve", lambda e, nt=nt: e.memset(Sb_[:, nt, :], 0.0), ["Sb"], ["Sb"])
            for c in range(nt):
                stt("dve", Sf[:, c + 1, :], Sf[:, c, :], gC[:, h:h + 1], pkvf[:, c, :], ALU.mult, ALU.add, ["Sf", "pkv"], ["Sf"])
            for c in range(nt - 1, -1, -1):
                stt("dve", Sb_[:, c, :], Sb_[:, c + 1, :], gC[:, 8 + h:9 + h], pkvb[:, c, :], ALU.mult, ALU.add, ["Sb", "pkv"], ["Sb"])
            cp("act", Sfb2[k][:, 0:nt, :], Sf[:, 0:nt, :], ["Sf"], ["Sfb%d" % k])
            cp("act", Sbb2[k][:, 0:nt, :], Sb_[:, 1:nt + 1, :], ["Sb"], ["Sbb%d" % k])
            if si < 2:
                store(nst_d[si, 0, h], Sf[:, nt, :], "Sf")
                store(nst_d[si, 1, h], Sb_[:, 0, :], "Sb")
            qv = QT[k][:, 0:L].rearrange("p (c i) -> p c i", i=128)
            tt("dve", Qxf2[k][:, 0:L].rearrange("p (c i) -> p c i", i=128), qv, bc(xibc[:, h, :], 1, [128, nt, 128]), ALU.mult, ["QT%d" % k], ["Qxf%d" % k])
            tt("dve", Qxb2[k][:, 0:L].rearrange("p (c i) -> p c i", i=128), qv, bc(xibc[:, 8 + h, :], 1, [128, nt, 128]), ALU.mult, ["QT%d" % k], ["Qxb%d" % k])

        def body(it):
            si, t0, nt, h = iters[it]
            k = it % 2
            Qxf = Qxf2[k]; Qxb = Qxb2[k]; Sfb = Sfb2[k]; Sbb = Sbb2[k]
            base = ccs["n"]; ccs["n"] += nt

            def smm2(c):
                r = (base + c) % 4
                cs_ = slice(c * 128, (c + 1) * 128)
                mms(pS[r], [(KT[k][:, cs_], QT[k][:, cs_])], ["KT%d" % k, "QT%d" % k], ["pS%d" % r])
            smm2(0)
            if nt > 1:
                smm2(1)
            for c in range(nt):
                    r = (base + c) % 4
                    cs_ = slice(c * 128, (c + 1) * 128)
                    if c + 2 < nt:
                        smm2(c + 2)
                    tt("dve", PT[r], pS[r], DT[:, h, :], ALU.mult, ["pS%d" % r], ["PT%d" % r])
                    mms(po[r], [(PT[r], Vc[k][:, c, :]), (Qxf[:, cs_], Sfb[:, c, :]), (Qxb[:, cs_], Sbb[:, c, :])],
                        ["PT%d" % r, "Vc%d" % k, "Qxf%d" % k, "Qxb%d" % k, "Sfb%d" % k, "Sbb%d" % k], ["po%d" % r])
                    S.op("dve", lambda e, r=r: e.bn_stats(out=bst[r], in_=po[r]), ["po%d" % r], ["bst%d" % r])
                    S.op("dve", lambda e, r=r: e.bn_aggr(out=mv[r], in_=bst[r]), ["bst%d" % r], ["mv%d" % r])
                    act(rs_[r], mv[r][:, 1:2], AF.Sqrt, ["mv%d" % r], ["rs%d" % r], bias=epsc[:, 0:1], scale=1.0)
                    S.op("dve", lambda e, r=r: e.reciprocal(out=rs_[r], in_=rs_[r]), ["rs%d" % r], ["rs%d" % r])
                    ts("dve", on_[r], po[r], mv[r][:, 0:1], ALU.subtract, ["po%d" % r, "mv%d" % r, "rs%d" % r], ["on%d" % r], s2=rs_[r][:, 0:1], op1=ALU.mult)
                    tt("dve", on_[r], on_[r], gng[:, h * 128:(h + 1) * 128], ALU.mult, ["on%d" % r, "gng"], ["on%d" % r])
                    tt("dve", og[r], on_[r], Gc[k][:, c, :], ALU.mult, ["on%d" % r, "Gc%d" % k], ["og%d" % r])
                    transposes([(ptr2[r], og[r])], ident_b, ["og%d" % r], ["ptr%d" % r])
                    cp("act", OTs[:, t0 + c, h, :], ptr2[r], ["ptr%d" % r], ["OTs"])

        prologue(0)
        for it in range(len(iters)):
            if it + 1 < len(iters):
                prologue(it + 1)
            body(it)
        S.barrier()
        st["off"] = P2_MARK

    wo_pref = {}
    if stop_after >= 3:
        if stop_after >= 4:
            wob_ = sb([16, D], BF16, top=True)
            wst_ = [sb([16, 128], F32, top=True) for _ in range(2)]
            wo_pref["wob"] = wob_
            wo_pref["n"] = 0

            def wo_step():
                cb = wo_pref["n"]
                if cb >= 16:
                    return
                wo_pref["n"] += 1
                k = cb % 2
                load(wst_[k], w_o[:, cb * 128:(cb + 1) * 128].rearrange("(c p) n -> p c n", p=128), "wstp%d" % k)
                cp("dve", wob_[:, :, cb * 128:(cb + 1) * 128], wst_[k], ["wstp%d" % k], ["wobp"])
            wo_pref["step"] = wo_step
        KTall = sb([4352], BF16); Vall = sb([34, 128], BF16)
        QTa = [sb([1024], BF16) for _ in range(2)]; PTa = [sb([512], BF16) for _ in range(2)]
        rec = sb([512]); ckf = sb([2, 256]); cvf = sb([2, 256]); ckb = sb([2, 256], BF16)
        load(ckf, ck_d.rearrange("(t p) c -> p t c", p=128), "ckf")
        load(cvf, cv_d.rearrange("(t p) c -> p t c", p=128), "cvf")
        cp("dve", ckb, ckf, ["ckf"], ["ckb"])
        pSa = [ps(0), ps(1)]
        poa = [ps(2), ps(4)]; pda = [ps(3), ps(5)]
        ptr3 = ps(6, 1, BF16)[:, 0:256].rearrange("p (a b) -> p a b", a=2)
        SCALE = 128.0 ** -0.5
        ia = 0; ib = 0
        for si, (t0, nt) in enumerate(((0, 2), (2, 2), (4, 8))):
            c0 = t0 * 128; L = nt * 128
            for g in range(2):
                if si == 2:
                    transposes([(ptr3[:, t, :], ckb[:, t, g * 128:(g + 1) * 128]) for t in range(2)], ident_b, ["ckb"], ["ptr3"])
                    cp("dve", KTall[:, 0:256], ptr3.rearrange("p a b -> p (a b)"), ["ptr3"], ["KTall"])
                    load(KTall[:, 256:1280], KAT[g, :, 512:1536], "KTall")
                    load(KTall[:, 1280:4352], KAT[g, :, 1536:4608], "KTall")
                    cp("dve", Vall[:, 0:2, :], cvf[:, :, g * 128:(g + 1) * 128], ["cvf"], ["Vall"])
                    load(Vall[:, 2:10, :], VA[4:12, :, g * 128:(g + 1) * 128].rearrange("t p c -> p t c"), "Vall")
                    for t8 in range(3):
                        load(Vall[:, 10 + t8 * 8:18 + t8 * 8, :], VA[12 + t8 * 8:20 + t8 * 8, :, g * 128:(g + 1) * 128].rearrange("t p c -> p t c"), "Vall")
                    nkc = 34
                else:
                    load(KTall[:, 0:256], KAT[g, :, c0:c0 + 256], "KTall")
                    load(Vall[:, 0:2, :], VA[t0:t0 + 2, :, g * 128:(g + 1) * 128].rearrange("t p c -> p t c"), "Vall")
                    nkc = 2
                for r_ in range(4):
                    hq = g * 4 + r_
                    k = ia % 2; ia += 1
                    load(QTa[k][:, 0:L], QAT[hq, :, c0:c0 + L], "QTa%d" % k)
                    if "step" in wo_pref:
                        wo_pref["step"]()
                    for q0 in range(0, L, 512):
                        qn = min(512, L - q0)
                        a_ = ib % 2; ib += 1
                        def smm(kc):
                            x = kc % 2
                            mms(pSa[x][:, 0:qn], [(KTall[:, kc * 128:(kc + 1) * 128], QTa[k][:, q0:q0 + qn])], ["KTall", "QTa%d" % k], ["pSa%d" % x])
                        smm(0)
                        for kc in range(nkc):
                            x = kc % 2
                            if kc + 1 < nkc:
                                smm(kc + 1)
                            act(PTa[x][:, 0:qn], pSa[x][:, 0:qn], AF.Exp, ["pSa%d" % x], ["PTa%d" % x], scale=SCALE)
                            mmx([(poa[a_][:, 0:qn], Vall[:, kc, :], PTa[x][:, 0:qn], kc == 0, kc == nkc - 1),
                                 (pda[a_][:, 0:qn], ones_b, PTa[x][:, 0:qn], kc == 0, kc == nkc - 1)], ["Vall", "PTa%d" % x, "ones_b"], ["poa%d" % a_])
                        S.op("dve", lambda e, a_=a_, qn=qn: e.reciprocal(out=rec[:, 0:qn], in_=pda[a_][:, 0:qn]), ["poa%d" % a_], ["rec"])
                        tq = t0 + q0 // 128
                        tt("dve", OTs[:, tq:tq + qn // 128, 8 + hq, :], poa[a_][:, 0:qn].rearrange("p (t i) -> p t i", i=128),
                           rec[:, 0:qn].rearrange("p (t i) -> p t i", i=128), ALU.mult, ["poa%d" % a_, "rec"], ["OTs"])
        while "step" in wo_pref and wo_pref["n"] < 16:
            wo_pref["step"]()
        for T in range(NT):
            store(OT[T], OTs[:, T, :, :], "OTs", ["OT"])
        S.barrier()
        st["off"] = P_LATE

    def layer_norm(xin, out, gam, bet, keyin, keyout, bst4, mv_, rs1):
        for q in range(4):
            S.op("dve", lambda e, q=q: e.bn_stats(out=bst4[:, q, :], in_=xin[:, q * 512:(q + 1) * 512]), [keyin], ["bst4"])
        S.op("dve", lambda e: e.bn_aggr(out=mv_, in_=bst4.rearrange("p a b -> p (a b)")), ["bst4"], ["mv_"])
        act(rs1, mv_[:, 1:2], AF.Sqrt, ["mv_"], ["rs1"], bias=epsc[:, 0:1], scale=1.0)
        S.op("dve", lambda e: e.reciprocal(out=rs1, in_=rs1), ["rs1"], ["rs1"])
        ts("dve", xin, xin, mv_[:, 0:1], ALU.subtract, [keyin, "mv_", "rs1"], [keyin], s2=rs1[:, 0:1], op1=ALU.mult)
        tt("dve", xin, xin, gam, ALU.mult, [keyin, "lng"], [keyin])
        tt("dve", out, xin, bet, ALU.add, [keyin, "lnb"], [keyout])

    if stop_after >= 4:
        wob = wo_pref["wob"]
        OTt = [sb([16, 128], BF16) for _ in range(2)]
        xt4 = [sb([D]) for _ in range(2)]
        g1bc = sb([D]); lng = sb([D]); lnb = sb([D]); xpre = sb([D]); x1t = [sb([D]) for _ in range(2)]
        h2t = [sb([16, 128], BF16) for _ in range(2)]
        bst4 = sb([4, 6]); mv4 = sb([2]); rs4 = sb([1])
        load(lng, ln1g_d.partition_broadcast(128), "lng"); load(lnb, ln1b_d.partition_broadcast(128), "lnb")
        pm4 = ps(0, 4); pT4 = ps(4, 4).rearrange("p (a b) -> p a b", a=16)
        def stage1(T):
            k = T % 2
            c_ = 0 if T < 4 else 1
            if T in (0, 4):
                load(g1bc, MODD[c_:c_ + 1, 2 * D:3 * D].partition_broadcast(128), "g1bc")
            load(OTt[k], OT[T], "OTt%d" % k, reads=["OT"])
            load(xt4[k], xown[T * 128:(T + 1) * 128, :], "xt4%d" % k)
            for q in range(4):
                mms(pm4[:, q * 512:(q + 1) * 512], [(OTt[k][:, fc, :], wob[:, fc, q * 512:(q + 1) * 512]) for fc in range(16)], ["OTt%d" % k, "wob"], ["pm4"])
            tt("dve", xpre, pm4, g1bc, ALU.mult, ["pm4", "g1bc"], ["xpre"])
            stt("dve", xpre, xt4[k], ALPHA, xpre, ALU.mult, ALU.add, ["xt4%d" % k, "xpre"], ["xpre"])
            layer_norm(xpre, x1t[k], lng, lnb, "xpre", "x1t%d" % k, bst4, mv4, rs4)
            store(X1[T], x1t[k], "x1t%d" % k, ["X1"])

        def stage2(T):
            k = T % 2
            c_ = 0 if T < 4 else 1
            transposes([(pT4[:, dk, :], x1t[k][:, dk * 128:(dk + 1) * 128]) for dk in range(16)], ident_f, ["x1t%d" % k], ["pT4"])
            for dk in range(16):
                act(h2t[k][:, dk, :], pT4[:, dk, :], AF.Identity, ["pT4"], ["h2t%d" % k], bias=modc[:, c_, 3, dk:dk + 1], scale=modc[:, c_, 2, dk:dk + 1])
            store(H2T[:, :, T * 128:(T + 1) * 128], h2t[k], "h2t%d" % k, ["H2T"])

        stage1(0)
        for T in range(NT):
            if T + 1 < NT:
                stage1(T + 1)
            stage2(T)
        S.barrier()
        st["off"] = P_LATE
        st["top"] = ARENA
    if "x1" in dbg:
        dbg_out["d_x1"] = dout("d_x1", [NT, 128, D]); dbg_out["d_ot"] = dout("d_ot", [NT, 128, 16, 128], BF16)
        dma("sp", dbg_out["d_x1"], X1, []); dma("sp", dbg_out["d_ot"], OT, [])

    if stop_after >= 5:
        wqb = sb([16, D], BF16)
        wst5 = [sb([16, 128]) for _ in range(2)]
        h2b = [sb([16, 512], BF16) for _ in range(2)]
        qT = sb([16, 512]); keysT = sb([16, 128]); ssb = [sb([D]) for _ in range(2)]
        pert = sb([16, 128])
        load(keysT, keysT_d.rearrange("h p n -> p h n"), "keysT")
        load(pert.rearrange("p a b -> p (a b)"), pert_d.partition_broadcast(128), "pert")
        load(h2b[0], H2T[:, :, 0:512], "h2b0")
        def wq_load(cb):
            load(wst5[cb % 2], w_q[:, cb * 128:(cb + 1) * 128].rearrange("(c p) n -> p c n", p=128), "wst5%d" % (cb % 2))

        def wq_cast(cb):
            k = cb % 2
            cp("dve" if k else "act", wqb[:, :, cb * 128:(cb + 1) * 128], wst5[k], ["wst5%d" % k], ["wqb%d" % cb])
        wq_load(0); wq_load(1)
        pq = [ps(0), ps(1)]
        psc = ps(4, 4).rearrange("p (a b) -> p a b", a=16)
        for tb in range(3):
            kb = tb % 2
            if tb > 0:
                load(h2b[kb], H2T[:, :, tb * 512:(tb + 1) * 512], "h2b%d" % kb)
            for hp in range(16):
                x = hp % 2
                if tb == 0:
                    wq_cast(hp)
                    if hp + 2 < 16:
                        wq_load(hp + 2)
                mms(pq[x], [(wqb[:, dk, hp * 128:(hp + 1) * 128], h2b[kb][:, dk, :]) for dk in range(16)], ["wqb%d" % hp, "h2b%d" % kb], ["pq%d" % x])
                cp("act" if x else "dve", qT[:, hp, :], pq[x], ["pq%d" % x], ["qT"])
            for tl in range(4):
                T = tb * 4 + tl
                mmx([(psc[:, hp, :], qT[:, hp, tl * 128:(tl + 1) * 128], keysT[:, hp, :], True, True) for hp in range(16)], ["qT", "keysT"], ["psc"])
                tt("dve", ssb[T % 2], psc.rearrange("p a b -> p (a b)"), pert.rearrange("p a b -> p (a b)"), ALU.add, ["psc", "pert"], ["ssb%d" % (T % 2)])
                store(SC[T], ssb[T % 2], "ssb%d" % (T % 2), ["SC"])
        S.barrier()
        st["off"] = P_LATE
    if "sc" in dbg:
        dbg_out["d_sc"] = dout("d_sc", [NT, 128, D])
        dma("sp", dbg_out["d_sc"], SC, [])

    if stop_after >= 5:
        NEG = -1.0e30
        s2b = [sb([16, 128]) for _ in range(2)]
        v_ = sb([16, 16]); cand = sb([8, 256]); cwork = sb([8, 256])
        work = cwork.rearrange("p h (a b) -> p (h a) b", a=2)
        best = sb([8, 16]); mm_ = sb([16]); tau = sb([8]); mx = sb([8]); Z = sb([8]); rZ = sb([8])
        ebest = sb([8, 16]); E2 = sb([8, 128]); r1 = sb([8, 16]); thr = sb([8, 16])
        E2b = sb([8, 128], BF16); r1b = sb([8, 16], BF16)
        tokA = sb([128, 128], BF16); tokB = sb([128, 128], BF16)
        M2T = sb([128, 128], BF16); O1T = sb([128, 128], BF16)
        Wt = sb([128, 128], BF16)
        v4 = v_.rearrange("p (h q) k -> p h q k", q=2)
        mm4 = mm_.rearrange("p (h q) -> p h q", q=2)
        tokAv = tokA.rearrange("p (h k) i -> p h k i", h=8); tokBv = tokB.rearrange("p (h k) i -> p h k i", h=8)
        ptrb = [ps(0, 1, BF16).rearrange("p (a b) -> p a b", a=8), ps(1, 1, BF16).rearrange("p (a b) -> p a b", a=8)]
        pw = [ps(2, 2).rearrange("p (a b) -> p a b", a=8), ps(4, 2).rearrange("p (a b) -> p a b", a=8)]
        SH = [128, 8, 16, 128]
        ie = 0
        def topk(T):
            sk = "s_%d" % (T % 2)
            s_ = s2b[T % 2]
            s4 = s_.rearrange("p (h q) n -> p h q n", q=2)
            load(s_.rearrange("p a b -> p (a b)"), SC[T], sk, reads=["SC"])
            for hp in range(16):
                S.op("dve", lambda e, hp=hp, s_=s_: e.max(out=v_[:, hp, 0:8], in_=s_[:, hp, :]), [sk], ["va%d" % hp])
            for hp in range(16):
                S.op("dve", lambda e, hp=hp, s_=s_: e.match_replace(out=work[:, hp, :], in_to_replace=v_[:, hp, 0:8], in_values=s_[:, hp, :], imm_value=NEG), [sk, "va%d" % hp], ["work%d" % hp])
            for hp in range(16):
                S.op("dve", lambda e, hp=hp: e.max(out=v_[:, hp, 8:16], in_=work[:, hp, :]), ["work%d" % hp], ["v_"])
            S.op("dve", lambda e: e.tensor_reduce(out=mm_, in_=v_, axis=AX.X, op=ALU.max), ["v_"], ["mm_"])
            tt("dve", cand.rearrange("p h (a b) -> p h a b", a=16), bc(v4[:, :, 0, :], 3, [128, 8, 16, 16]), bc(v4[:, :, 1, :], 2, [128, 8, 16, 16]), ALU.add, ["v_"], ["cand"])
            for h in range(8):
                S.op("dve", lambda e, h=h: e.max(out=best[:, h, 0:8], in_=cand[:, h, :]), ["cand"], ["ba%d" % h])
            for h in range(8):
                S.op("dve", lambda e, h=h: e.match_replace(out=cwork[:, h, :], in_to_replace=best[:, h, 0:8], in_values=cand[:, h, :], imm_value=NEG), ["cand", "ba%d" % h], ["cw%d" % h])
            for h in range(8):
                S.op("dve", lambda e, h=h: e.max(out=best[:, h, 8:16], in_=cwork[:, h, :]), ["cw%d" % h], ["best"])
            S.op("dve", lambda e: e.tensor_reduce(out=tau, in_=best, axis=AX.X, op=ALU.min), ["best"], ["tau"])
            tt("dve", mx, mm4[:, :, 0], mm4[:, :, 1], ALU.add, ["mm_"], ["mx"])
            tt("dve", ebest, best, bc(mx, 2, [128, 8, 16]), ALU.subtract, ["best", "mx"], ["ebest"])
            act(ebest, ebest, AF.Exp, ["ebest"], ["ebest"])
            tt("dve", E2, s4[:, :, 1, :], bc(mm4[:, :, 1], 2, [128, 8, 128]), ALU.subtract, [sk, "mm_"], ["E2"])
            act(E2b, E2, AF.Exp, ["E2"], ["E2b"])
            tt("dve", r1, v4[:, :, 0, :], bc(mm4[:, :, 0], 2, [128, 8, 16]), ALU.subtract, ["v_", "mm_"], ["r1"])
            act(r1, r1, AF.Exp, ["r1"], ["r1"])
            cw4 = cwork.rearrange("p h (a b) -> p h a b", a=16)
            tt("dve", cw4, cand.rearrange("p h (a b) -> p h a b", a=16), tau.unsqueeze(2).unsqueeze(3).to_broadcast([128, 8, 16, 16]), ALU.is_lt, ["cand", "tau"], ["cwork"])
            ts("dve", cw4, cw4, 1.0e9, ALU.mult, ["cwork"], ["cwork"])
            tt("dve", cw4, cw4, bc(v4[:, :, 1, :], 2, [128, 8, 16, 16]), ALU.max, ["cwork", "v_"], ["cwork"])
            S.op("dve", lambda e: e.tensor_reduce(out=thr, in_=cw4, axis=AX.X, op=ALU.min), ["cwork"], ["thr"])
            S.op("dve", lambda e: e.tensor_reduce(out=Z, in_=ebest, axis=AX.X, op=ALU.add), ["ebest"], ["Z"])
            S.op("dve", lambda e: e.reciprocal(out=rZ, in_=Z), ["Z"], ["rZ"])
            tt("dve", r1b, r1, bc(rZ, 2, [128, 8, 16]), ALU.mult, ["r1", "rZ"], ["r1b"])

        def passes(T):
            sk = "s_%d" % (T % 2)
            s4 = s2b[T % 2].rearrange("p (h q) n -> p h q n", q=2)
            tt("dve", tokAv, bc(s4[:, :, 1, :], 2, SH), bc(thr, 3, SH), ALU.is_ge, [sk, "thr"], ["tokA"])
            tt("dve", tokAv, tokAv, bc(E2b, 2, SH), ALU.mult, ["tokA", "E2b"], ["tokA"])
            tt("dve", tokBv, bc(s4[:, :, 0, :], 2, SH), bc(v4[:, :, 0, :], 3, SH), ALU.is_equal, [sk, "v_"], ["tokB"])
            tt("dve", tokBv, tokBv, bc(r1b, 3, SH), ALU.mult, ["tokB", "r1b"], ["tokB"])

        iec = {"n": 0}

        def transp(T):
            for src, sk_, dst, dk_ in ((tokA, "tokA", M2T, "M2T"), (tokB, "tokB", O1T, "O1T")):
                for i8 in range(16):
                    x = iec["n"] % 2; iec["n"] += 1
                    transposes([(ptrb[x][:, a, :], src[:, :, i8 * 8 + a]) for a in range(8)], ident_b, [sk_], ["ptrb%d" % x])
                    cp("act", dst[:, i8 * 8:(i8 + 1) * 8, :], ptrb[x], ["ptrb%d" % x], [dk_])

        def tokmm(T):
            for t8 in range(16):
                x = t8 % 2
                mmx([(pw[x][:, a, :], M2T[:, :, t8 * 8 + a], O1T[:, :, t8 * 8 + a], True, True) for a in range(8)], ["M2T", "O1T"], ["pw%d" % x])
                cp("act", Wt[:, :, t8 * 8:(t8 + 1) * 8], pw[x].rearrange("p a i -> p i a"), ["pw%d" % x], ["Wt"])
            for c8 in range(16):
                dma("sp", WD[T // 6, c8 * 8:(c8 + 1) * 8, :, (T % 6) * 128:(T % 6 + 1) * 128].rearrange("c p t -> p c t"), Wt[:, c8 * 8:(c8 + 1) * 8, :], ["Wt"], ["WD"])

        topk(0); passes(0)
        for T in range(NT):
            transp(T)
            if T + 1 < NT:
                topk(T + 1)
            tokmm(T)
            if T + 1 < NT:
                passes(T + 1)
        S.barrier()
        st["off"] = P_LATE
    if "wd" in dbg:
        dbg_out["d_wd"] = dout("d_wd", [128, 128, 128], BF16)
        dma("sp", dbg_out["d_wd"], WD[0, :, :, 0:128], [])

    if stop_after >= 6:
        G = 4
        h2 = sb([16, 768], BF16)
        acc = sb([6, D])
        P6_MARK = st["off"]
        NCH = 128
        NB = 3
        for hf in range(2):
            st["off"] = P6_MARK
            ust = [sb([16, 128]) for _ in range(NB)]; ub = [sb([16, 128], BF16) for _ in range(2)]
            vst = [sb([D]) for _ in range(NB)]; vb = [sb([G, D], BF16) for _ in range(2)]
            wt = [sb([6, 128], BF16) for _ in range(NB)]; gl = [sb([768], BF16) for _ in range(2)]
            ST = [sb([G, 768], BF16) for _ in range(2)]
            hc0 = 0
            load(h2, H2T[:, :, hf * 768:(hf + 1) * 768], "h2")
            pa = [[ps(0)[:, 0:384], ps(1)[:, 0:384]], [ps(2)[:, 0:384], ps(3)[:, 0:384]]]
            pb = [ps(4), ps(5), ps(6), ps(7)]

            def ld(c):
                k = c % NB
                load(ust[k], UT_d[c].rearrange("p (k e) -> p k e", k=16), "ust%d" % k)
                load(vst[k], V_d[c * 128:(c + 1) * 128, :], "vst%d" % k)
                load(wt[k].rearrange("p t i -> p (t i)"), WD[hf, c], "wt%d" % k, reads=["WD"])

            ib = {"n": 0}

            def phaseB(grp):
                g2 = grp % 2
                for tl in range(6):
                    for db in range(4):
                        x = ib["n"] % 4; ib["n"] += 1
                        mms(pb[x], [(ST[g2][:, ci, tl * 128:(tl + 1) * 128], vb[g2][:, ci, db * 512:(db + 1) * 512]) for ci in range(G)],
                            ["ST%d" % g2, "vb%d" % g2], ["pb%d" % x])
                        if grp == 0:
                            cp("dve", acc[:, tl, db * 512:(db + 1) * 512], pb[x], ["pb%d" % x], ["acc"])
                        else:
                            tt("dve", acc[:, tl, db * 512:(db + 1) * 512], acc[:, tl, db * 512:(db + 1) * 512], pb[x], ALU.add, ["pb%d" % x, "acc"], ["acc"])

            def casts(c):
                k = c % 2; k3 = c % NB; g2 = (c // G) % 2; ci = c % G
                cp("act", ub[k], ust[k3], ["ust%d" % k3], ["ub%d" % k])
                cp("act", vb[g2][:, ci, :], vst[k3], ["vst%d" % k3], ["vb%d" % g2])

            ld(0); ld(1)
            casts(0)
            for c in range(NCH):
                k = c % 2; k3 = c % NB; grp = c // G; g2 = grp % 2; ci = c % G
                if c + 2 < NCH:
                    ld(c + 2)
                if c + 1 < NCH:
                    casts(c + 1)
                for blk in range(2):
                    mms(pa[k][blk], [(ub[k][:, dk, :], h2[:, dk, hc0 + blk * 384:hc0 + (blk + 1) * 384]) for dk in range(16)], ["ub%d" % k, "h2"], ["pa%d%d" % (k, blk)])
                    act(gl[k][:, blk * 384:(blk + 1) * 384], pa[k][blk], AF.Gelu, ["pa%d%d" % (k, blk)], ["gl%d" % k])
                tt("dve", ST[g2][:, ci, :], gl[k], wt[k3].rearrange("p t i -> p (t i)"), ALU.mult, ["gl%d" % k, "wt%d" % k3], ["ST%d" % g2])
                if ci == 0 and grp > 0:
                    phaseB(grp - 1)
            phaseB(NCH // G - 1)
            S.barrier()
            st["off"] = P6_MARK
            x1t = [sb([D]) for _ in range(2)]; g2bc = sb([D]); lng = sb([D]); lnb = sb([D]); yt = [sb([D]) for _ in range(2)]
            bst4 = sb([4, 6]); mv4 = sb([2]); rs4 = sb([1])
            load(lng, ln2g_d.partition_broadcast(128), "lng"); load(lnb, ln2b_d.partition_broadcast(128), "lnb")
            for tl in range(6):
                T = hf * 6 + tl; k = tl % 2
                c_ = 0 if T < 4 else 1
                if tl == 0 or T == 4:
                    load(g2bc, MODD[c_:c_ + 1, 5 * D:6 * D].partition_broadcast(128), "g2bc")
                load(x1t[k], X1[T], "x1t%d" % k)
                tt("dve", acc[:, tl, :], acc[:, tl, :], g2bc, ALU.mult, ["acc", "g2bc"], ["acc"])
                stt("dve", x1t[k], x1t[k], ALPHA, acc[:, tl, :], ALU.mult, ALU.add, ["x1t%d" % k, "acc"], ["x1t%d" % k])
                layer_norm(x1t[k], yt[k], lng, lnb, "x1t%d" % k, "yt%d" % k, bst4, mv4, rs4)
                store(y_d[T * 128:(T + 1) * 128, :], yt[k], "yt%d" % k)
            S.barrier()

    if "p1" in dbg:
        for nm, src, shp in (("d_QRT", QRT, [8, 128, NT * 128]), ("d_KR", KR, [NT + NO, 128, 1024]), ("d_QAT", QAT, [8, 128, NT * 128]),
                             ("d_KAT", KAT, [2, 128, (NT + NO) * 128]), ("d_VA", VA, [NT + NO, 128, 256]), ("d_GR", GR, [NT, 128, 1024])):
            dbg_out[nm] = dout(nm, shp, BF16)
            dma("sp", dbg_out[nm], src, [])

    S.barrier()
    S.emit()
    es.close()
    return dbg_out


def _rope_tables(pos):
    row = (pos // 64).astype(np.float32); col = (pos % 64).astype(np.float32)
    inv = (10000.0 ** (-np.arange(0, 64, 2, dtype=np.float32) / 64.0)).astype(np.float32)
    ar = row[:, None] * inv; ac = col[:, None] * inv
    ang = np.concatenate([ar, ar, ac, ac], -1).astype(np.float32)
    sgn = np.concatenate([-np.ones(32), np.ones(32), -np.ones(32), np.ones(32)]).astype(np.float32)
    return np.cos(ang).astype(np.float32), (np.sin(ang) * sgn).astype(np.float32)


def make_in_maps(inp):
    f = lambda a: np.ascontiguousarray(a, dtype=np.float32)
    UT = f(inp["peer_u"][0].reshape(128, 128, 16, 128).transpose(0, 3, 2, 1)).reshape(128, 128, D)
    keysT = f(np.transpose(inp["peer_sub_keys"][0], (0, 1, 3, 2)).reshape(16, 128, 128))
    shared = dict(
        w_ada=f(inp["w_ada"][0]), b_ada=f(inp["b_ada"]), w_in=f(inp["w_in"][0]), decay=f(inp["ret_decay_logit"][0].reshape(1, 16)),
        gn_g=f(inp["ret_gn_g"]), qn_g=f(inp["q_norm_g"]), kn_g=f(inp["k_norm_g"]), w_o=f(inp["w_o"][0]),
        ln1_g=f(inp["ln1_g"]), ln1_b=f(inp["ln1_b"]), w_q=f(inp["peer_w_q"][0]), keysT=keysT, UT=UT, V=f(inp["peer_v"][0]),
        ln2_g=f(inp["ln2_g"]), ln2_b=f(inp["ln2_b"]), ident=np.eye(128, dtype=np.float32),
        pert=np.tile(-4e-6 * np.arange(128, dtype=np.float32), 16).reshape(1, 2048),
    )
    i_ = np.arange(128)
    j_ = np.arange(128)
    dij = (i_[None, :] - j_[:, None]).astype(np.float32)
    shared["A1"] = np.maximum(dij, 0); shared["A2"] = np.maximum(-dij, 0)
    shared["U1"] = (dij >= 0).astype(np.float32); shared["L1"] = (dij <= 0).astype(np.float32)
    shared["zexp"] = np.stack([127.0 - j_, j_ * 1.0], 1).astype(np.float32)
    shared["xirow"] = np.stack([i_ + 1.0, 128.0 - i_], 0).astype(np.float32)
    maps = []
    for c in range(8):
        b, qd = c // 4, c % 4
        a, e = qd * 1024, (qd + 1) * 1024
        own = np.arange(a, e); oth = np.concatenate([np.arange(0, a), np.arange(e, 4096)])
        m = dict(shared)
        m["xown"] = f(np.concatenate([inp["x_prompt"][2 * c], inp["x_prompt"][2 * c + 1], inp["x_sample"][b, own]], 0))
        m["xoth"] = f(inp["x_sample"][b, oth])
        m["cond"] = f(np.stack([inp["c_ctx"], inp["c"][b]], 0))
        m["s0"] = f(inp["state_ret"][b, 0].reshape(16, 128, 128))
        m["ck"] = f(inp["cache_k"][b, 0].reshape(256, 256)); m["cv"] = f(inp["cache_v"][b, 0].reshape(256, 256))
        m["cos_own"], m["sin_own"] = _rope_tables(own)
        m["cos_oth"], m["sin_oth"] = _rope_tables(oth)
        ef = np.where(oth < a, a - 1 - oth, 0).astype(np.float32); mf = (oth < a).astype(np.float32)
        eb = np.where(oth >= e, oth - e, 0).astype(np.float32); mb = (oth >= e).astype(np.float32)
        m["oexp"] = f(np.stack([ef.reshape(NO, 128).T, eb.reshape(NO, 128).T], 1))
        m["omsk"] = f(np.stack([mf.reshape(NO, 128).T, mb.reshape(NO, 128).T], 1))
        m["s0c"] = np.array([[a, 4096 - e]], dtype=np.float32)
        maps.append(m)
    return maps


def kernel(**inp):
    inp = {k: np.asarray(v) for k, v in inp.items()}
    nc = bass.Bass("TRN2", target_bir_lowering=False)
    build(nc)
    maps = make_in_maps(inp)
    res = run_bass_kernel_spmd(nc, maps, core_ids=list(range(8)))
    R = res.results
    y_p = np.zeros((16, 256, D), np.float32); y_s = np.zeros((2, 4096, D), np.float32)
    nst = np.zeros((16, 1, 2, 8, 128, 128), np.float32); nk = np.zeros((16, 1, 256, 2, 128), np.float32); nv = np.zeros_like(nk)
    for c in range(8):
        b, qd = c // 4, c % 4
        y = R[c]["y"]
        y_p[2 * c] = y[0:256]; y_p[2 * c + 1] = y[256:512]
        y_s[b, qd * 1024:(qd + 1) * 1024] = y[512:]
        nst[2 * c:2 * c + 2, 0] = R[c]["nst"]
        nk[2 * c:2 * c + 2, 0] = R[c]["nk"].reshape(2, 256, 2, 128)
        nv[2 * c:2 * c + 2, 0] = R[c]["nv"].reshape(2, 256, 2, 128)
    return y_p, y_s, nst, nk, nv
```

```python
import contextlib
import numpy as np
import concourse.bass as bass
import concourse.mybir as mybir
from concourse.bass_utils import run_bass_kernel_spmd

F32 = mybir.dt.float32
BF16 = mybir.dt.bfloat16
F32R = mybir.dt.float32r
ALU = mybir.AluOpType
AF = mybir.ActivationFunctionType
AX = mybir.AxisListType

D = 2048
NT = 12
NO = 24
EPS = 1e-5
ALPHA = 2.0 ** 0.25
ENGS = ("pe", "act", "dve", "pool", "sp")
N_DMA_SEMS = 24


class Sched:
    def __init__(self, nc):
        self.nc = nc
        self.items = {e: [] for e in ENGS}
        self.count = {}
        self.known = {e: {} for e in ENGS}
        self.last_w = {}
        self.readers = {}
        self.dma_i = 0
        self.nops = 0

    def _need(self, eng, tok):
        sk, v = tok
        if self.known[eng].get(sk, 0) >= v:
            return
        if sk == eng and eng == "pe":
            return
        self.items[eng].append(("w", sk, v))
        self.known[eng][sk] = v

    def op(self, eng, fn, reads=(), writes=(), dma=False):
        deps = []
        for r in reads:
            if r in self.last_w:
                deps.append(self.last_w[r])
        for w in writes:
            if w in self.last_w and self.last_w[w][0] != eng:
                deps.append(self.last_w[w])
            deps.extend(t for t in self.readers.get(w, ()) if t[0] != eng)
        best = {}
        for sk, v in deps:
            if best.get(sk, 0) < v:
                best[sk] = v
        for sk, v in best.items():
            self._need(eng, (sk, v))
        if dma:
            sk = ("dma", self.dma_i % N_DMA_SEMS)
            self.dma_i += 1
            prev = self.count.get(sk, 0)
            if prev:
                self._need(eng, (sk, prev))
            val, inc = prev + 16, 16
        else:
            sk = eng
            val, inc = self.count.get(sk, 0) + 1, 1
        self.count[sk] = val
        tok = (sk, val)
        self.items[eng].append(("o", fn, sk, inc))
        for r in reads:
            self.readers.setdefault(r, []).append(tok)
        for w in writes:
            self.last_w[w] = tok
            self.readers[w] = []
        self.nops += 1
        return tok

    def barrier(self):
        for e in ENGS:
            for sk, v in list(self.count.items()):
                self._need(e, (sk, v))
        self.last_w = {}
        self.readers = {}

    def emit(self):
        nc = self.nc
        with contextlib.ExitStack() as es:
            sems = {}
            for sk in self.count:
                name = "s_" + (sk if isinstance(sk, str) else "dma%d" % sk[1])
                sems[sk] = es.enter_context(nc.semaphore(name))
            block = es.enter_context(nc.Block())
            engmap = {"pe": block.tensor, "act": block.scalar, "dve": block.vector,
                      "pool": block.gpsimd, "sp": block.sync}

            def mk(items):
                def body(eng):
                    for it in items:
                        if it[0] == "w":
                            eng.wait_ge(sems[it[1]], it[2])
                        else:
                            it[1](eng).then_inc(sems[it[2]], it[3])
                return body

            for e in ENGS:
                if self.items[e]:
                    engmap[e](mk(self.items[e]))


def build(nc, stop_after=99, dbg=()):
    es = contextlib.ExitStack()
    S = Sched(nc)
    dbg_out = {}

    def din(name, shape, dt=F32):
        return nc.dram_tensor(name, list(shape), dt, kind="ExternalInput").ap()

    def dscr(name, shape, dt=BF16):
        return nc.dram_tensor(name, list(shape), dt, kind="Internal").ap()

    def dout(name, shape, dt=F32):
        return nc.dram_tensor(name, list(shape), dt, kind="ExternalOutput").ap()

    xown = din("xown", [NT * 128, D]); xoth = din("xoth", [NO * 128, D])
    cond_d = din("cond", [2, D]); s0_d = din("s0", [16, 128, 128])
    ck_d = din("ck", [256, 256]); cv_d = din("cv", [256, 256])
    w_ada = din("w_ada", [D, 6 * D]); b_ada = din("b_ada", [1, 6 * D]); w_in = din("w_in", [D, 5632])
    decay_d = din("decay", [1, 16]); gng_d = din("gn_g", [1, 1024]); qng_d = din("qn_g", [1, 128]); kng_d = din("kn_g", [1, 128])
    w_o = din("w_o", [D, D]); ln1g_d = din("ln1_g", [1, D]); ln1b_d = din("ln1_b", [1, D])
    w_q = din("w_q", [D, D]); keysT_d = din("keysT", [16, 128, 128]); UT_d = din("UT", [128, 128, D]); V_d = din("V", [16384, D])
    ln2g_d = din("ln2_g", [1, D]); ln2b_d = din("ln2_b", [1, D])
    ident_d = din("ident", [128, 128]); pert_d = din("pert", [1, 2048])
    cos_o = din("cos_own", [1024, 128]); sin_o = din("sin_own", [1024, 128])
    cos_x = din("cos_oth", [3072, 128]); sin_x = din("sin_oth", [3072, 128])
    A1_d = din("A1", [128, 128]); A2_d = din("A2", [128, 128]); U1_d = din("U1", [128, 128]); L1_d = din("L1", [128, 128])
    zexp_d = din("zexp", [128, 2]); xirow_d = din("xirow", [2, 128])
    oexp_d = din("oexp", [128, 2, NO]); omsk_d = din("omsk", [128, 2, NO]); s0c_d = din("s0c", [1, 2])
    y_d = dout("y", [NT * 128, D]); nst_d = dout("nst", [2, 2, 8, 128, 128])
    nk_d = dout("nk", [2, 256, 256]); nv_d = dout("nv", [2, 256, 256])
    MODD = dscr("MODD", [2, 6 * D], F32)
    QRT = dscr("QRT", [8, 128, NT * 128]); KRT = dscr("KRT", [8, 128, NT * 128])
    KR = dscr("KR", [NT + NO, 128, 1024]); VR = dscr("VR", [NT + NO, 128, 1024]); GR = dscr("GR", [NT, 128, 1024])
    QAT = dscr("QAT", [8, 128, NT * 128]); KAT = dscr("KAT", [2, 128, (NT + NO) * 128]); VA = dscr("VA", [NT + NO, 128, 256])
    OT = dscr("OT", [NT, 128, 16, 128]); X1 = dscr("X1", [NT, 128, D], F32); H2T = dscr("H2T", [128, 16, NT * 128])
    SC = dscr("SC", [NT, 128, D], F32); WD = dscr("WD", [2, 128, 128, 768])

    ARENA = 105000
    arena_t = es.enter_context(nc.sbuf_tensor("arena", [128, ARENA], BF16))
    psum_t = es.enter_context(nc.psum_tensor("psum", [128, 4096], F32))
    st = {"off": 0, "top": ARENA}

    def sb(shape, dt=F32, top=False):
        n = int(np.prod(shape))
        nb = n * 2 if dt == F32 else n
        nb = (nb + 15) // 16 * 16
        if top:
            st["top"] -= nb
            assert st["top"] >= st["off"], ("SBUF arena overflow (top)", st["off"], st["top"])
            ap = arena_t[:, st["top"]:st["top"] + nb]
        else:
            assert st["off"] + nb <= st["top"], ("SBUF arena overflow", st["off"], nb, st["top"])
            ap = arena_t[:, st["off"]:st["off"] + nb]
            st["off"] += nb
        if dt == F32:
            ap = ap.bitcast(F32)[:, 0:n]
        else:
            ap = ap[:, 0:n]
        if len(shape) == 2:
            ap = ap.rearrange("p (a b) -> p a b", a=shape[0], b=shape[1])
        elif len(shape) == 3:
            ap = ap.rearrange("p (a b c) -> p a b c", a=shape[0], b=shape[1], c=shape[2])
        elif len(shape) == 4:
            ap = ap.rearrange("p (a b c d) -> p a b c d", a=shape[0], b=shape[1], c=shape[2], d=shape[3])
        return ap

    def ps(bank, nbanks=1, dt=F32):
        ap = psum_t[:, bank * 512:(bank + nbanks) * 512]
        if dt == BF16:
            ap = ap.bitcast(BF16)
        return ap

    def dma(eng, out, in_, reads=(), writes=(), slow=False):
        if slow:
            fn = lambda e: e.dma_start(out=out, in_=in_, allow_slow_non_contiguous=True)
        else:
            fn = lambda e: e.dma_start(out=out, in_=in_)
        S.op(eng, fn, reads, writes, dma=True)

    def load(out, in_, w, reads=(), slow=False):
        dma("sp", out, in_, reads, [w] if isinstance(w, str) else w, slow)

    def store(out, in_, r, writes=(), slow=False):
        dma("sp", out, in_, [r] if isinstance(r, str) else r, writes, slow)

    def tt(eng, out, in0, in1, op, reads, writes):
        S.op(eng, lambda e: e.tensor_tensor(out=out, in0=in0, in1=in1, op=op), reads, writes)

    def ts(eng, out, in0, s1, op0, reads, writes, s2=None, op1=None):
        if op1 is None:
            S.op(eng, lambda e: e.tensor_single_scalar(out=out, in_=in0, scalar=s1, op=op0), reads, writes)
        else:
            S.op(eng, lambda e: e.tensor_scalar(out=out, in0=in0, scalar1=s1, scalar2=s2, op0=op0, op1=op1), reads, writes)

    def stt(eng, out, in0, scalar, in1, op0, op1, reads, writes):
        S.op(eng, lambda e: e.scalar_tensor_tensor(out=out, in0=in0, scalar=scalar, in1=in1, op0=op0, op1=op1), reads, writes)

    def act(out, in_, func, reads, writes, bias=None, scale=None, accum=None):
        kw = {}
        if bias is not None:
            kw["bias"] = bias
        if scale is not None:
            kw["scale"] = scale
        if accum is not None:
            kw["accum_out"] = accum
        S.op("act", lambda e: e.activation(out=out, in_=in_, func=func, **kw), reads, writes)

    def cp(eng, out, in_, reads, writes):
        if eng == "act":
            S.op("act", lambda e: e.copy(out=out, in_=in_), reads, writes)
        else:
            S.op(eng, lambda e: e.tensor_copy(out=out, in_=in_), reads, writes)

    def mms(out, pairs, reads, writes):
        def fn(e):
            n = len(pairs)
            for i, (l, r) in enumerate(pairs):
                ins = e.matmul(out, lhsT=l, rhs=r, start=(i == 0), stop=(i == n - 1))
            return ins
        S.op("pe", fn, reads, writes)

    def mmx(ops, reads, writes):
        def fn(e):
            for o, l, r, a, b in ops:
                ins = e.matmul(o, lhsT=l, rhs=r, start=a, stop=b)
            return ins
        S.op("pe", fn, reads, writes)

    def transposes(items, ident, reads, writes):
        def fn(e):
            for o, i in items:
                ins = e.transpose(out=o, in_=i, identity=ident)
            return ins
        S.op("pe", fn, reads, writes)

    def bc(ap, axis, shape):
        return ap.unsqueeze(axis).to_broadcast(list(shape))

    ident_f = sb([128]); ident_b = sb([128], BF16); ones_b = sb([128], BF16)
    modc = sb([2, 4, 16])
    epsc = sb([1])
    P_LATE = st["off"]
    lg = sb([16])
    gC = sb([16]); cs0 = sb([16]); zz = sb([16])
    xibc = sb([16, 128]); DT = sb([8, 128])
    wo = sb([2, NO, 8])
    P_MARK = st["off"]

    load(ident_f, ident_d, "ident_f")
    cp("dve", ident_b, ident_f, ["ident_f"], ["ident_b"])
    S.op("dve", lambda e: e.memset(ones_b, 1.0), [], ["ones_b"])
    S.op("dve", lambda e: e.memset(epsc, EPS), [], ["epsc"])

    def phase0():
        condT = sb([16, 2]); scT = sb([16, 2]); bada = sb([6 * D])
        wa = [sb([16, 512]) for _ in range(2)]
        mrow = [sb([512]) for _ in range(2)]
        for k_ in range(2):
            load(condT[:, :, k_], cond_d[k_, :].rearrange("(c p) -> p c", p=128), "condT", slow=True)
        act(scT, condT, AF.Silu, ["condT"], ["scT"])
        load(bada[0:2, :], b_ada.partition_broadcast(2), "bada")
        pm = [ps(0), ps(1)]
        load(wa[0], w_ada[:, 0:512].rearrange("(c p) n -> p c n", p=128), "wa0")
        for cb in range(24):
            k = cb % 2
            if cb + 1 < 24:
                load(wa[1 - k], w_ada[:, (cb + 1) * 512:(cb + 2) * 512].rearrange("(c p) n -> p c n", p=128), "wa%d" % (1 - k))
            mms(pm[k][0:2, :], [(scT[:, dk, :], wa[k][:, dk, :]) for dk in range(16)], ["scT", "wa%d" % k], ["pm%d" % k])
            tt("dve", mrow[k][0:2, :], pm[k][0:2, :], bada[0:2, cb * 512:(cb + 1) * 512], ALU.add, ["pm%d" % k, "bada"], ["mrow%d" % k])
            store(MODD[:, cb * 512:(cb + 1) * 512], mrow[k][0:2, :], "mrow%d" % k, ["MODD"])
        for wi, off in enumerate((1 * D, 0, 4 * D, 3 * D)):
            for k_ in range(2):
                load(modc[:, k_, wi, :], MODD[k_, off:off + D].rearrange("(c p) -> p c", p=128), "modc%d" % wi, reads=["MODD"], slow=True)
        for wi in (0, 2):
            ts("dve", modc[:, :, wi, :], modc[:, :, wi, :], 1.0, ALU.add, ["modc%d" % wi], ["modc%d" % wi])
        dl = sb([16]); y_ = sb([16]); u_ = sb([16]); lnp = sb([16]); msk = sb([16])
        zexp = sb([2]); xirow = sb([2, 128]); s0c = sb([2]); A1 = sb([128]); A2 = sb([128]); U1 = sb([128]); L1 = sb([128])
        oexp = sb([2, NO]); omsk = sb([2, NO]); e1 = sb([128]); e2 = sb([128]); wtmp = sb([NO, 8])
        load(dl, decay_d.partition_broadcast(128), "dl")
        load(zexp, zexp_d, "zexp"); load(s0c, s0c_d.partition_broadcast(128), "s0c")
        for r in range(2):
            load(xirow[:, r, :], xirow_d[r:r + 1, :].partition_broadcast(128), "xirow")
        load(A1, A1_d, "A1"); load(A2, A2_d, "A2"); load(U1, U1_d, "U1"); load(L1, L1_d, "L1")
        load(oexp, oexp_d, "oexp"); load(omsk, omsk_d, "omsk")
        act(y_, dl, AF.Exp, ["dl"], ["y_"], scale=-1.0)
        ts("dve", u_, y_, -1.0 / 8, ALU.mult, ["y_"], ["u_"])
        for k in range(7, 0, -1):
            stt("dve", u_, u_, ((-1.0) ** (k + 1)) / k, y_, ALU.add, ALU.mult, ["u_", "y_"], ["u_"])
        act(lnp, y_, AF.Ln, ["y_"], ["lnp"], bias=1.0)
        ts("dve", msk, y_, 0.3, ALU.is_le, ["y_"], ["msk"])
        tt("dve", u_, u_, lnp, ALU.subtract, ["u_", "lnp"], ["u_"])
        tt("dve", u_, u_, msk, ALU.mult, ["u_", "msk"], ["u_"])
        tt("dve", u_, u_, lnp, ALU.add, ["u_", "lnp"], ["u_"])
        ts("dve", lg, u_, -1.0, ALU.mult, ["u_"], ["lg"])
        act(gC, lg, AF.Exp, ["lg"], ["gC"], scale=128.0)
        for d_ in range(2):
            act(cs0[:, d_ * 8:(d_ + 1) * 8], lg[:, d_ * 8:(d_ + 1) * 8], AF.Exp, ["lg", "s0c"], ["cs0"], scale=s0c[:, d_:d_ + 1])
            act(zz[:, d_ * 8:(d_ + 1) * 8], lg[:, d_ * 8:(d_ + 1) * 8], AF.Exp, ["lg", "zexp"], ["zz"], scale=zexp[:, d_:d_ + 1])
            for h in range(8):
                act(xibc[:, d_ * 8 + h, :], xirow[:, d_, :], AF.Exp, ["lg", "xirow"], ["xibc"], scale=lg[:, d_ * 8 + h:d_ * 8 + h + 1])
            tt("dve", wtmp, bc(oexp[:, d_, :], 2, [128, NO, 8]), bc(lg[:, d_ * 8:(d_ + 1) * 8], 1, [128, NO, 8]), ALU.mult, ["oexp", "lg"], ["wtmp"])
            act(wtmp, wtmp, AF.Exp, ["wtmp"], ["wtmp"])
            tt("dve", wo[:, d_, :, :], wtmp, bc(omsk[:, d_, :], 2, [128, NO, 8]), ALU.mult, ["wtmp", "omsk"], ["wo"])
        for h in range(8):
            act(e1, A1, AF.Exp, ["A1", "lg"], ["e1"], scale=lg[:, h:h + 1])
            tt("dve", e1, e1, U1, ALU.mult, ["e1", "U1"], ["e1"])
            act(e2, A2, AF.Exp, ["A2", "lg"], ["e2"], scale=lg[:, 8 + h:9 + h])
            tt("dve", e2, e2, L1, ALU.mult, ["e2", "L1"], ["e2"])
            tt("dve", DT[:, h, :], e1, e2, ALU.add, ["e1", "e2"], ["DT"])

    phase0()
    S.barrier()
    st["off"] = P_MARK
    if "mod" in dbg:
        dbg_out["d_modc"] = dout("d_modc", [128, 128]); dbg_out["d_ret"] = dout("d_ret", [128, 64])
        store(dbg_out["d_modc"], modc.rearrange("p a b c -> p (a b c)"), [])
        store(dbg_out["d_ret"][:, 0:16], lg, []); store(dbg_out["d_ret"][:, 16:32], gC, [])
        store(dbg_out["d_ret"][:, 32:48], cs0, []); store(dbg_out["d_ret"][:, 48:64], zz, [])

    def rope(eng, dst, src, cs, nh, rk, wk):
        tmp = rope_tmp[:, 0:nh, :]
        tv = tmp.rearrange("p h (a s d) -> p h a s d", a=2, s=2)
        sv = src.rearrange("p h (a s d) -> p h a s d", a=2, s=2)
        snv = cs[:, 1, :].rearrange("p (a s d) -> p a s d", a=2, s=2)
        for s_ in range(2):
            tt(eng, tv[:, :, :, s_, :], sv[:, :, :, 1 - s_, :], bc(snv[:, :, s_, :], 1, [128, nh, 2, 32]), ALU.mult, [rk, "cs"], ["rope_tmp"])
        tt(eng, src, src, bc(cs[:, 0, :], 1, [128, nh, 128]), ALU.mult, [rk, "cs", "rope_tmp"], [rk])
        tt(eng, dst, src, tmp, ALU.add, [rk, "rope_tmp"], [wk])

    if stop_after >= 1:
        hT = sb([NT, 16, 128], BF16)
        xt = [sb([D]) for _ in range(2)]
        wst = [sb([16, 512]) for _ in range(2)]
        wb = [sb([16, 512], BF16) for _ in range(2)]
        ebf = [sb([512], BF16) for _ in range(2)]
        trs = [sb([4, 128], BF16) for _ in range(2)]
        nrm = [sb([4, 128]) for _ in range(2)]
        nbf = [sb([4, 128], BF16) for _ in range(2)]
        vf = [sb([256]) for _ in range(2)]
        cs_t = [sb([2, 128]) for _ in range(2)]
        rope_tmp = sb([4, 128])
        ssq = sb([8]); rstd = sb([8]); junk = sb([128])
        qg = sb([128]); kg = sb([128])
        load(qg, qng_d.partition_broadcast(128), "qg"); load(kg, kng_d.partition_broadcast(128), "kg")
        pT = ps(0, 4).rearrange("p (a b) -> p a b", a=16)
        pp = [ps(4), ps(5)]
        ptr = [ps(6, 1, BF16)[:, 0:512].rearrange("p (a b) -> p a b", a=4), ps(7, 1, BF16)[:, 0:512].rearrange("p (a b) -> p a b", a=4)]
        cnt = {"e": 0, "x": 0}

        def rmsnorm(src_ps, nh, gain, dst, rk, wk):
            for hh in range(nh):
                act(junk, src_ps[:, hh * 128:(hh + 1) * 128], AF.Square, [rk, "junk"], ["junk", "ssq"], accum=ssq[:, hh:hh + 1])
            act(rstd[:, 0:nh], ssq[:, 0:nh], AF.Sqrt, ["ssq", "epsc"], ["rstd"], bias=epsc[:, 0:1], scale=1.0 / 128)
            S.op("dve", lambda e: e.reciprocal(out=rstd[:, 0:nh], in_=rstd[:, 0:nh]), ["rstd"], ["rstd"])
            for hh in range(nh):
                stt("dve", dst[:, hh, :], src_ps[:, hh * 128:(hh + 1) * 128], rstd[:, hh:hh + 1], gain, ALU.mult, ALU.mult,
                    [rk, "rstd", "qg", "kg"], [wk])

        groups = [(list(range(0, 12)), list(range(11))), (list(range(12, 24)), [2, 3, 4, 5, 10]), (list(range(24, 36)), [2, 3, 4, 5, 10])]
        for tiles, blocks in groups:
            for li, T in enumerate(tiles):
                k = cnt["x"] % 2; cnt["x"] += 1
                src = xown[T * 128:(T + 1) * 128, :] if T < NT else xoth[(T - NT) * 128:(T - NT + 1) * 128, :]
                load(xt[k], src, "xt%d" % k)
                transposes([(pT[:, dk, :], xt[k][:, dk * 128:(dk + 1) * 128]) for dk in range(16)], ident_f, ["xt%d" % k, "ident_f"], ["pT"])
                c_ = 0 if T < 4 else 1
                for dk in range(16):
                    act(hT[:, li, dk, :], pT[:, dk, :], AF.Identity, ["pT", "modc0", "modc1"], ["hT%d" % li],
                        bias=modc[:, c_, 1, dk:dk + 1], scale=modc[:, c_, 0, dk:dk + 1])
            def wload(bi):
                cb = blocks[bi]
                load(wst[bi % 2], w_in[:, cb * 512:(cb + 1) * 512].rearrange("(c p) n -> p c n", p=128), "wst%d" % (bi % 2))
            wload(0)
            for bi, cb in enumerate(blocks):
                kb = bi % 2
                cp("dve", wb[kb][:, 0:8, :], wst[kb][:, 0:8, :], ["wst%d" % kb], ["wb%da" % kb])
                cp("act", wb[kb][:, 8:16, :], wst[kb][:, 8:16, :], ["wst%d" % kb], ["wb%db" % kb])
                if bi + 1 < len(blocks):
                    wload(bi + 1)
                pending = []

                def tr_store(k, items, srckey, nh, dst):
                    def f_():
                        transposes([(ptr[k][:, hh, :], it) for hh, it in enumerate(items)], ident_b, [srckey, "ident_b"], ["ptr%d" % k])
                        cp("dve", trs[k][:, 0:nh, :], ptr[k][:, 0:nh, :], ["ptr%d" % k], ["trs%d" % k])
                        store(dst, trs[k][:, 0:nh, :], "trs%d" % k, ["scr"])
                    pending.append(f_)

                for li, T in enumerate(tiles):
                    k = cnt["e"] % 2; cnt["e"] += 1
                    P = pp[k]; pk = "pp%d" % k
                    own = T < NT
                    mms(P, [(hT[:, li, dk, :], wb[kb][:, dk, :]) for dk in range(16)], ["hT%d" % li, "wb%da" % kb, "wb%db" % kb], [pk])
                    for f_ in pending:
                        f_()
                    del pending[:]
                    E = ebf[k]; ek = "ebf%d" % k
                    if cb in (0, 1):
                        cp("act", E, P, [pk], [ek])
                        tr_store(k, [E[:, hh * 128:(hh + 1) * 128] for hh in range(4)], ek, 4,
                                 QRT[cb * 4:cb * 4 + 4, :, T * 128:(T + 1) * 128].rearrange("h p t -> p h t"))
                    elif cb in (2, 3):
                        act(E, P, AF.Copy, [pk], [ek], scale=128.0 ** -0.5)
                        store(KR[T, :, (cb - 2) * 512:(cb - 1) * 512], E, ek, ["KR"])
                        if own:
                            tr_store(k, [E[:, hh * 128:(hh + 1) * 128] for hh in range(4)], ek, 4,
                                     KRT[(cb - 2) * 4:(cb - 2) * 4 + 4, :, T * 128:(T + 1) * 128].rearrange("h p t -> p h t"))
                    elif cb in (4, 5):
                        cp("act", E, P, [pk], [ek])
                        store(VR[T, :, (cb - 4) * 512:(cb - 3) * 512], E, ek, ["VR"])
                    elif cb in (6, 7):
                        act(E, P, AF.Silu, [pk], [ek])
                        store(GR[T, :, (cb - 6) * 512:(cb - 5) * 512], E, ek, ["GR"])
                    elif cb in (8, 9):
                        rmsnorm(P, 4, qg, nrm[k], pk, "nrm%d" % k)
                        if T >= 4:
                            load(cs_t[k][:, 0, :], cos_o[(T - 4) * 128:(T - 3) * 128, :], "cs")
                            load(cs_t[k][:, 1, :], sin_o[(T - 4) * 128:(T - 3) * 128, :], "cs")
                            rope("dve", nbf[k], nrm[k], cs_t[k], 4, "nrm%d" % k, "nbf%d" % k)
                        else:
                            cp("dve", nbf[k], nrm[k], ["nrm%d" % k], ["nbf%d" % k])
                        tr_store(k, [nbf[k][:, hh, :] for hh in range(4)], "nbf%d" % k, 4,
                                 QAT[(cb - 8) * 4:(cb - 8) * 4 + 4, :, T * 128:(T + 1) * 128].rearrange("h p t -> p h t"))
                    else:
                        rmsnorm(P, 2, kg, nrm[k], pk, "nrm%d" % k)
                        if T < 4:
                            sq, r0 = T // 2, (T % 2) * 128
                            store(nk_d[sq, r0:r0 + 128, :], nrm[k][:, 0:2, :].rearrange("p h d -> p (h d)"), "nrm%d" % k)
                            cp("act", vf[k], P[:, 256:512], [pk], ["vf%d" % k])
                            store(nv_d[sq, r0:r0 + 128, :], vf[k], "vf%d" % k)
                            cp("dve", nbf[k][:, 0:2, :], nrm[k][:, 0:2, :], ["nrm%d" % k], ["nbf%d" % k])
                        else:
                            ctab, stab, r0 = (cos_o, sin_o, (T - 4) * 128) if own else (cos_x, sin_x, (T - NT) * 128)
                            load(cs_t[k][:, 0, :], ctab[r0:r0 + 128, :], "cs")
                            load(cs_t[k][:, 1, :], stab[r0:r0 + 128, :], "cs")
                            rope("dve", nbf[k][:, 0:2, :], nrm[k][:, 0:2, :], cs_t[k], 2, "nrm%d" % k, "nbf%d" % k)
                        tr_store(k, [nbf[k][:, hh, :] for hh in range(2)], "nbf%d" % k, 2,
                                 KAT[:, :, T * 128:(T + 1) * 128].rearrange("h p t -> p h t"))
                        cp("act", E[:, 0:256], P[:, 256:512], [pk], [ek])
                        store(VA[T], E[:, 0:256], ek, ["VA"])
                for f_ in pending:
                    f_()
                del pending[:]
        S.barrier()
        st["off"] = P_MARK


    if stop_after >= 2:
        OTs = sb([NT, 16, 128], BF16)
        P2_MARK = st["off"]
        KRo = sb([NO, 1024], BF16); VRo = sb([NO, 1024], BF16)
        kw = [sb([NO, 128], BF16) for _ in range(2)]
        s0t = sb([16, 128]); Sent = sb([16, 128])
        for t in range(NO):
            load(KRo[:, t, :], KR[NT + t], "KRo")
            load(VRo[:, t, :], VR[NT + t], "VRo")
        load(s0t, s0_d.rearrange("i p v -> p i v"), "s0t")
        accS = ps(0, 4).rearrange("p (a b) -> p a b", a=16)
        for idx in range(16):
            d_, h = idx // 8, idx % 8
            k = idx % 2
            tt("dve", kw[k], KRo[:, :, h * 128:(h + 1) * 128], bc(wo[:, d_, :, h], 2, [128, NO, 128]), ALU.mult, ["KRo"], ["kw%d" % k])
            mms(accS[:, idx, :], [(kw[k][:, t, :], VRo[:, t, h * 128:(h + 1) * 128]) for t in range(NO)], ["kw%d" % k, "VRo"], ["accS%d" % idx])
            stt("dve", Sent[:, idx, :], s0t[:, idx, :], cs0[:, idx:idx + 1], accS[:, idx, :], ALU.mult, ALU.add, ["s0t", "accS%d" % idx], ["Sent"])
        S.barrier()
        if "sent" in dbg:
            dbg_out["d_sent"] = dout("d_sent", [128, 16, 128])
            store(dbg_out["d_sent"], Sent, [])
            S.barrier()
        st["off"] = P2_MARK
        Sent2 = sb([16, 128])
        cp("dve", Sent2, Sent, [], ["Sent2"])
        S.barrier()
        Sent = Sent2
        QT = [sb([1024], BF16) for _ in range(2)]; KT = [sb([1024], BF16) for _ in range(2)]
        Kc = [sb([8, 128], BF16) for _ in range(2)]; Vc = [sb([8, 128], BF16) for _ in range(2)]; Gc = [sb([8, 128], BF16) for _ in range(2)]
        kzf = sb([8, 128], BF16); kzb = sb([8, 128], BF16)
        Sf = sb([9, 128]); Sb_ = sb([9, 128]); Sfb = sb([8, 128], BF16); Sbb = sb([8, 128], BF16)
        Qxf = sb([1024], BF16); Qxb = sb([1024], BF16)
        PT = [sb([128], BF16) for _ in range(4)]; on_ = [sb([128]) for _ in range(4)]; og = [sb([128], BF16) for _ in range(4)]
        bst = [sb([6]) for _ in range(4)]; mv = [sb([2]) for _ in range(4)]; rs_ = [sb([1]) for _ in range(4)]
        gng = sb([1024])
        load(gng, gng_d.partition_broadcast(128), "gng")
        pkvf = ps(0, 2).rearrange("p (a b) -> p a b", a=8); pkvb = ps(2, 2).rearrange("p (a b) -> p a b", a=8)
        pS = [ps(4)[:, q * 128:(q + 1) * 128] for q in range(4)]
        po = [ps(5)[:, q * 128:(q + 1) * 128] for q in range(4)]
        ptr2 = [ps(6, 1, BF16)[:, q * 128:(q + 1) * 128] for q in range(4)]
        Sfb2 = [Sfb, sb([8, 128], BF16)]; Sbb2 = [Sbb, sb([8, 128], BF16)]
        Qxf2 = [Qxf, sb([1024], BF16)]; Qxb2 = [Qxb, sb([1024], BF16)]
        iters = [(si, t0, nt, h) for si, (t0, nt) in enumerate(((0, 2), (2, 2), (4, 8))) for h in range(8)]
        ccs = {"n": 0}

        def prologue(it):
            si, t0, nt, h = iters[it]
            k = it % 2
            L = nt * 128; c0 = t0 * 128
            load(QT[k][:, 0:L], QRT[h, :, c0:c0 + L], "QT%d" % k)
            load(KT[k][:, 0:L], KRT[h, :, c0:c0 + L], "KT%d" % k)
            load(Kc[k][:, 0:nt, :], KR[t0:t0 + nt, :, h * 128:(h + 1) * 128].rearrange("t p c -> p t c"), "Kc%d" % k)
            load(Vc[k][:, 0:nt, :], VR[t0:t0 + nt, :, h * 128:(h + 1) * 128].rearrange("t p c -> p t c"), "Vc%d" % k)
            load(Gc[k][:, 0:nt, :], GR[t0:t0 + nt, :, h * 128:(h + 1) * 128].rearrange("t p c -> p t c"), "Gc%d" % k)
            ts("dve", kzf[:, 0:nt, :], Kc[k][:, 0:nt, :], zz[:, h:h + 1], ALU.mult, ["Kc%d" % k], ["kzf"])
            ts("dve", kzb[:, 0:nt, :], Kc[k][:, 0:nt, :], zz[:, 8 + h:9 + h], ALU.mult, ["Kc%d" % k], ["kzb"])
            mmx([(pkvf[:, c, :], kzf[:, c, :], Vc[k][:, c, :], True, True) for c in range(nt)] +
                [(pkvb[:, c, :], kzb[:, c, :], Vc[k][:, c, :], True, True) for c in range(nt)], ["kzf", "kzb", "Vc%d" % k], ["pkv"])
            if si == 2:
                cp("dve", Sf[:, 0, :], Sent[:, h, :], ["Sent2"], ["Sf"])
                cp("dve", Sb_[:, nt, :], Sent[:, 8 + h, :], ["Sent2"], ["Sb"])
            else:
                S.op("dve", lambda e: e.memset(Sf[:, 0, :], 0.0), ["Sf"], ["Sf"])
                S.op("dve", lambda e, nt=nt: e.memset(Sb_[:, nt, :], 0.0), ["Sb"], ["Sb"])
            for c in range(nt):
                stt("dve", Sf[:, c + 1, :], Sf[:, c, :], gC[:, h:h + 1], pkvf[:, c, :], ALU.mult, ALU.add, ["Sf", "pkv"], ["Sf"])
            for c in range(nt - 1, -1, -1):
                stt("dve", Sb_[:, c, :], Sb_[:, c + 1, :], gC[:, 8 + h:9 + h], pkvb[:, c, :], ALU.mult, ALU.add, ["Sb", "pkv"], ["Sb"])
            cp("act", Sfb2[k][:, 0:nt, :], Sf[:, 0:nt, :], ["Sf"], ["Sfb%d" % k])
            cp("act", Sbb2[k][:, 0:nt, :], Sb_[:, 1:nt + 1, :], ["Sb"], ["Sbb%d" % k])
            if si < 2:
                store(nst_d[si, 0, h], Sf[:, nt, :], "Sf")
                store(nst_d[si, 1, h], Sb_[:, 0, :], "Sb")
            qv = QT[k][:, 0:L].rearrange("p (c i) -> p c i", i=128)
            tt("dve", Qxf2[k][:, 0:L].rearrange("p (c i) -> p c i", i=128), qv, bc(xibc[:, h, :], 1, [128, nt, 128]), ALU.mult, ["QT%d" % k], ["Qxf%d" % k])
            tt("dve", Qxb2[k][:, 0:L].rearrange("p (c i) -> p c i", i=128), qv, bc(xibc[:, 8 + h, :], 1, [128, nt, 128]), ALU.mult, ["QT%d" % k], ["Qxb%d" % k])

        def body(it):
            si, t0, nt, h = iters[it]
            k = it % 2
            Qxf = Qxf2[k]; Qxb = Qxb2[k]; Sfb = Sfb2[k]; Sbb = Sbb2[k]
            base = ccs["n"]; ccs["n"] += nt

            def smm2(c):
                r = (base + c) % 4
                cs_ = slice(c * 128, (c + 1) * 128)
                mms(pS[r], [(KT[k][:, cs_], QT[k][:, cs_])], ["KT%d" % k, "QT%d" % k], ["pS%d" % r])
            smm2(0)
            if nt > 1:
                smm2(1)
            for c in range(nt):
                    r = (base + c) % 4
                    cs_ = slice(c * 128, (c + 1) * 128)
                    if c + 2 < nt:
                        smm2(c + 2)
                    tt("dve", PT[r], pS[r], DT[:, h, :], ALU.mult, ["pS%d" % r], ["PT%d" % r])
                    mms(po[r], [(PT[r], Vc[k][:, c, :]), (Qxf[:, cs_], Sfb[:, c, :]), (Qxb[:, cs_], Sbb[:, c, :])],
                        ["PT%d" % r, "Vc%d" % k, "Qxf%d" % k, "Qxb%d" % k, "Sfb%d" % k, "Sbb%d" % k], ["po%d" % r])
                    S.op("dve", lambda e, r=r: e.bn_stats(out=bst[r], in_=po[r]), ["po%d" % r], ["bst%d" % r])
                    S.op("dve", lambda e, r=r: e.bn_aggr(out=mv[r], in_=bst[r]), ["bst%d" % r], ["mv%d" % r])
                    act(rs_[r], mv[r][:, 1:2], AF.Sqrt, ["mv%d" % r], ["rs%d" % r], bias=epsc[:, 0:1], scale=1.0)
                    S.op("dve", lambda e, r=r: e.reciprocal(out=rs_[r], in_=rs_[r]), ["rs%d" % r], ["rs%d" % r])
                    ts("dve", on_[r], po[r], mv[r][:, 0:1], ALU.subtract, ["po%d" % r, "mv%d" % r, "rs%d" % r], ["on%d" % r], s2=rs_[r][:, 0:1], op1=ALU.mult)
                    tt("dve", on_[r], on_[r], gng[:, h * 128:(h + 1) * 128], ALU.mult, ["on%d" % r, "gng"], ["on%d" % r])
                    tt("dve", og[r], on_[r], Gc[k][:, c, :], ALU.mult, ["on%d" % r, "Gc%d" % k], ["og%d" % r])
                    transposes([(ptr2[r], og[r])], ident_b, ["og%d" % r], ["ptr%d" % r])
                    cp("act", OTs[:, t0 + c, h, :], ptr2[r], ["ptr%d" % r], ["OTs"])

        prologue(0)
        for it in range(len(iters)):
            if it + 1 < len(iters):
                prologue(it + 1)
            body(it)
        S.barrier()
        st["off"] = P2_MARK

    wo_pref = {}
    if stop_after >= 3:
        if stop_after >= 4:
            wob_ = sb([16, D], BF16, top=True)
            wst_ = [sb([16, 128], F32, top=True) for _ in range(2)]
            wo_pref["wob"] = wob_
            wo_pref["n"] = 0

            def wo_step():
                cb = wo_pref["n"]
                if cb >= 16:
                    return
                wo_pref["n"] += 1
                k = cb % 2
                load(wst_[k], w_o[:, cb * 128:(cb + 1) * 128].rearrange("(c p) n -> p c n", p=128), "wstp%d" % k)
                cp("dve", wob_[:, :, cb * 128:(cb + 1) * 128], wst_[k], ["wstp%d" % k], ["wobp"])
            wo_pref["step"] = wo_step
        KTall = sb([4352], BF16); Vall = sb([34, 128], BF16)
        QTa = [sb([1024], BF16) for _ in range(2)]; PTa = [sb([512], BF16) for _ in range(2)]
        rec = sb([512]); ckf = sb([2, 256]); cvf = sb([2, 256]); ckb = sb([2, 256], BF16)
        load(ckf, ck_d.rearrange("(t p) c -> p t c", p=128), "ckf")
        load(cvf, cv_d.rearrange("(t p) c -> p t c", p=128), "cvf")
        cp("dve", ckb, ckf, ["ckf"], ["ckb"])
        pSa = [ps(0), ps(1)]
        poa = [ps(2), ps(4)]; pda = [ps(3), ps(5)]
        ptr3 = ps(6, 1, BF16)[:, 0:256].rearrange("p (a b) -> p a b", a=2)
        SCALE = 128.0 ** -0.5
        ia = 0; ib = 0
        for si, (t0, nt) in enumerate(((0, 2), (2, 2), (4, 8))):
            c0 = t0 * 128; L = nt * 128
            for g in range(2):
                if si == 2:
                    transposes([(ptr3[:, t, :], ckb[:, t, g * 128:(g + 1) * 128]) for t in range(2)], ident_b, ["ckb"], ["ptr3"])
                    cp("dve", KTall[:, 0:256], ptr3.rearrange("p a b -> p (a b)"), ["ptr3"], ["KTall"])
                    load(KTall[:, 256:1280], KAT[g, :, 512:1536], "KTall")
                    load(KTall[:, 1280:4352], KAT[g, :, 1536:4608], "KTall")
                    cp("dve", Vall[:, 0:2, :], cvf[:, :, g * 128:(g + 1) * 128], ["cvf"], ["Vall"])
                    load(Vall[:, 2:10, :], VA[4:12, :, g * 128:(g + 1) * 128].rearrange("t p c -> p t c"), "Vall")
                    for t8 in range(3):
                        load(Vall[:, 10 + t8 * 8:18 + t8 * 8, :], VA[12 + t8 * 8:20 + t8 * 8, :, g * 128:(g + 1) * 128].rearrange("t p c -> p t c"), "Vall")
                    nkc = 34
                else:
                    load(KTall[:, 0:256], KAT[g, :, c0:c0 + 256], "KTall")
                    load(Vall[:, 0:2, :], VA[t0:t0 + 2, :, g * 128:(g + 1) * 128].rearrange("t p c -> p t c"), "Vall")
                    nkc = 2
                for r_ in range(4):
                    hq = g * 4 + r_
                    k = ia % 2; ia += 1
                    load(QTa[k][:, 0:L], QAT[hq, :, c0:c0 + L], "QTa%d" % k)
                    if "step" in wo_pref:
                        wo_pref["step"]()
                    for q0 in range(0, L, 512):
                        qn = min(512, L - q0)
                        a_ = ib % 2; ib += 1
                        def smm(kc):
                            x = kc % 2
                            mms(pSa[x][:, 0:qn], [(KTall[:, kc * 128:(kc + 1) * 128], QTa[k][:, q0:q0 + qn])], ["KTall", "QTa%d" % k], ["pSa%d" % x])
                        smm(0)
                        for kc in range(nkc):
                            x = kc % 2
                            if kc + 1 < nkc:
                                smm(kc + 1)
                            act(PTa[x][:, 0:qn], pSa[x][:, 0:qn], AF.Exp, ["pSa%d" % x], ["PTa%d" % x], scale=SCALE)
                            mmx([(poa[a_][:, 0:qn], Vall[:, kc, :], PTa[x][:, 0:qn], kc == 0, kc == nkc - 1),
                                 (pda[a_][:, 0:qn], ones_b, PTa[x][:, 0:qn], kc == 0, kc == nkc - 1)], ["Vall", "PTa%d" % x, "ones_b"], ["poa%d" % a_])
                        S.op("dve", lambda e, a_=a_, qn=qn: e.reciprocal(out=rec[:, 0:qn], in_=pda[a_][:, 0:qn]), ["poa%d" % a_], ["rec"])
                        tq = t0 + q0 // 128
                        tt("dve", OTs[:, tq:tq + qn // 128, 8 + hq, :], poa[a_][:, 0:qn].rearrange("p (t i) -> p t i", i=128),
                           rec[:, 0:qn].rearrange("p (t i) -> p t i", i=128), ALU.mult, ["poa%d" % a_, "rec"], ["OTs"])
        while "step" in wo_pref and wo_pref["n"] < 16:
            wo_pref["step"]()
        for T in range(NT):
            store(OT[T], OTs[:, T, :, :], "OTs", ["OT"])
        S.barrier()
        st["off"] = P_LATE

    def layer_norm(xin, out, gam, bet, keyin, keyout, bst4, mv_, rs1):
        for q in range(4):
            S.op("dve", lambda e, q=q: e.bn_stats(out=bst4[:, q, :], in_=xin[:, q * 512:(q + 1) * 512]), [keyin], ["bst4"])
        S.op("dve", lambda e: e.bn_aggr(out=mv_, in_=bst4.rearrange("p a b -> p (a b)")), ["bst4"], ["mv_"])
        act(rs1, mv_[:, 1:2], AF.Sqrt, ["mv_"], ["rs1"], bias=epsc[:, 0:1], scale=1.0)
        S.op("dve", lambda e: e.reciprocal(out=rs1, in_=rs1), ["rs1"], ["rs1"])
        ts("dve", xin, xin, mv_[:, 0:1], ALU.subtract, [keyin, "mv_", "rs1"], [keyin], s2=rs1[:, 0:1], op1=ALU.mult)
        tt("dve", xin, xin, gam, ALU.mult, [keyin, "lng"], [keyin])
        tt("dve", out, xin, bet, ALU.add, [keyin, "lnb"], [keyout])

    if stop_after >= 4:
        wob = wo_pref["wob"]
        OTt = [sb([16, 128], BF16) for _ in range(2)]
        xt4 = [sb([D]) for _ in range(2)]
        g1bc = sb([D]); lng = sb([D]); lnb = sb([D]); xpre = sb([D]); x1t = [sb([D]) for _ in range(2)]
        h2t = [sb([16, 128], BF16) for _ in range(2)]
        bst4 = sb([4, 6]); mv4 = sb([2]); rs4 = sb([1])
        load(lng, ln1g_d.partition_broadcast(128), "lng"); load(lnb, ln1b_d.partition_broadcast(128), "lnb")
        pm4 = ps(0, 4); pT4 = ps(4, 4).rearrange("p (a b) -> p a b", a=16)
        def stage1(T):
            k = T % 2
            c_ = 0 if T < 4 else 1
            if T in (0, 4):
                load(g1bc, MODD[c_:c_ + 1, 2 * D:3 * D].partition_broadcast(128), "g1bc")
            load(OTt[k], OT[T], "OTt%d" % k, reads=["OT"])
            load(xt4[k], xown[T * 128:(T + 1) * 128, :], "xt4%d" % k)
            for q in range(4):
                mms(pm4[:, q * 512:(q + 1) * 512], [(OTt[k][:, fc, :], wob[:, fc, q * 512:(q + 1) * 512]) for fc in range(16)], ["OTt%d" % k, "wob"], ["pm4"])
            tt("dve", xpre, pm4, g1bc, ALU.mult, ["pm4", "g1bc"], ["xpre"])
            stt("dve", xpre, xt4[k], ALPHA, xpre, ALU.mult, ALU.add, ["xt4%d" % k, "xpre"], ["xpre"])
            layer_norm(xpre, x1t[k], lng, lnb, "xpre", "x1t%d" % k, bst4, mv4, rs4)
            store(X1[T], x1t[k], "x1t%d" % k, ["X1"])

        def stage2(T):
            k = T % 2
            c_ = 0 if T < 4 else 1
            transposes([(pT4[:, dk, :], x1t[k][:, dk * 128:(dk + 1) * 128]) for dk in range(16)], ident_f, ["x1t%d" % k], ["pT4"])
            for dk in range(16):
                act(h2t[k][:, dk, :], pT4[:, dk, :], AF.Identity, ["pT4"], ["h2t%d" % k], bias=modc[:, c_, 3, dk:dk + 1], scale=modc[:, c_, 2, dk:dk + 1])
            store(H2T[:, :, T * 128:(T + 1) * 128], h2t[k], "h2t%d" % k, ["H2T"])

        stage1(0)
        for T in range(NT):
            if T + 1 < NT:
                stage1(T + 1)
            stage2(T)
        S.barrier()
        st["off"] = P_LATE
        st["top"] = ARENA
    if "x1" in dbg:
        dbg_out["d_x1"] = dout("d_x1", [NT, 128, D]); dbg_out["d_ot"] = dout("d_ot", [NT, 128, 16, 128], BF16)
        dma("sp", dbg_out["d_x1"], X1, []); dma("sp", dbg_out["d_ot"], OT, [])

    if stop_after >= 5:
        wqb = sb([16, D], BF16)
        wst5 = [sb([16, 128]) for _ in range(2)]
        h2b = [sb([16, 512], BF16) for _ in range(2)]
        qT = sb([16, 512]); keysT = sb([16, 128]); ssb = [sb([D]) for _ in range(2)]
        pert = sb([16, 128])
        load(keysT, keysT_d.rearrange("h p n -> p h n"), "keysT")
        load(pert.rearrange("p a b -> p (a b)"), pert_d.partition_broadcast(128), "pert")
        load(h2b[0], H2T[:, :, 0:512], "h2b0")
        def wq_load(cb):
            load(wst5[cb % 2], w_q[:, cb * 128:(cb + 1) * 128].rearrange("(c p) n -> p c n", p=128), "wst5%d" % (cb % 2))

        def wq_cast(cb):
            k = cb % 2
            cp("dve" if k else "act", wqb[:, :, cb * 128:(cb + 1) * 128], wst5[k], ["wst5%d" % k], ["wqb%d" % cb])
        wq_load(0); wq_load(1)
        pq = [ps(0), ps(1)]
        psc = ps(4, 4).rearrange("p (a b) -> p a b", a=16)
        for tb in range(3):
            kb = tb % 2
            if tb > 0:
                load(h2b[kb], H2T[:, :, tb * 512:(tb + 1) * 512], "h2b%d" % kb)
            for hp in range(16):
                x = hp % 2
                if tb == 0:
                    wq_cast(hp)
                    if hp + 2 < 16:
                        wq_load(hp + 2)
                mms(pq[x], [(wqb[:, dk, hp * 128:(hp + 1) * 128], h2b[kb][:, dk, :]) for dk in range(16)], ["wqb%d" % hp, "h2b%d" % kb], ["pq%d" % x])
                cp("act" if x else "dve", qT[:, hp, :], pq[x], ["pq%d" % x], ["qT"])
            for tl in range(4):
                T = tb * 4 + tl
                mmx([(psc[:, hp, :], qT[:, hp, tl * 128:(tl + 1) * 128], keysT[:, hp, :], True, True) for hp in range(16)], ["qT", "keysT"], ["psc"])
                tt("dve", ssb[T % 2], psc.rearrange("p a b -> p (a b)"), pert.rearrange("p a b -> p (a b)"), ALU.add, ["psc", "pert"], ["ssb%d" % (T % 2)])
                store(SC[T], ssb[T % 2], "ssb%d" % (T % 2), ["SC"])
        S.barrier()
        st["off"] = P_LATE
    if "sc" in dbg:
        dbg_out["d_sc"] = dout("d_sc", [NT, 128, D])
        dma("sp", dbg_out["d_sc"], SC, [])

    if stop_after >= 5:
        NEG = -1.0e30
        s2b = [sb([16, 128]) for _ in range(2)]
        v_ = sb([16, 16]); cand = sb([8, 256]); cwork = sb([8, 256])
        work = cwork.rearrange("p h (a b) -> p (h a) b", a=2)
        best = sb([8, 16]); mm_ = sb([16]); tau = sb([8]); mx = sb([8]); Z = sb([8]); rZ = sb([8])
        ebest = sb([8, 16]); E2 = sb([8, 128]); r1 = sb([8, 16]); thr = sb([8, 16])
        E2b = sb([8, 128], BF16); r1b = sb([8, 16], BF16)
        tokA = sb([128, 128], BF16); tokB = sb([128, 128], BF16)
        M2T = sb([128, 128], BF16); O1T = sb([128, 128], BF16)
        Wt = sb([128, 128], BF16)
        v4 = v_.rearrange("p (h q) k -> p h q k", q=2)
        mm4 = mm_.rearrange("p (h q) -> p h q", q=2)
        tokAv = tokA.rearrange("p (h k) i -> p h k i", h=8); tokBv = tokB.rearrange("p (h k) i -> p h k i", h=8)
        ptrb = [ps(0, 1, BF16).rearrange("p (a b) -> p a b", a=8), ps(1, 1, BF16).rearrange("p (a b) -> p a b", a=8)]
        pw = [ps(2, 2).rearrange("p (a b) -> p a b", a=8), ps(4, 2).rearrange("p (a b) -> p a b", a=8)]
        SH = [128, 8, 16, 128]
        ie = 0
        def topk(T):
            sk = "s_%d" % (T % 2)
            s_ = s2b[T % 2]
            s4 = s_.rearrange("p (h q) n -> p h q n", q=2)
            load(s_.rearrange("p a b -> p (a b)"), SC[T], sk, reads=["SC"])
            for hp in range(16):
                S.op("dve", lambda e, hp=hp, s_=s_: e.max(out=v_[:, hp, 0:8], in_=s_[:, hp, :]), [sk], ["va%d" % hp])
            for hp in range(16):
                S.op("dve", lambda e, hp=hp, s_=s_: e.match_replace(out=work[:, hp, :], in_to_replace=v_[:, hp, 0:8], in_values=s_[:, hp, :], imm_value=NEG), [sk, "va%d" % hp], ["work%d" % hp])
            for hp in range(16):
                S.op("dve", lambda e, hp=hp: e.max(out=v_[:, hp, 8:16], in_=work[:, hp, :]), ["work%d" % hp], ["v_"])
            S.op("dve", lambda e: e.tensor_reduce(out=mm_, in_=v_, axis=AX.X, op=ALU.max), ["v_"], ["mm_"])
            tt("dve", cand.rearrange("p h (a b) -> p h a b", a=16), bc(v4[:, :, 0, :], 3, [128, 8, 16, 16]), bc(v4[:, :, 1, :], 2, [128, 8, 16, 16]), ALU.add, ["v_"], ["cand"])
            for h in range(8):
                S.op("dve", lambda e, h=h: e.max(out=best[:, h, 0:8], in_=cand[:, h, :]), ["cand"], ["ba%d" % h])
            for h in range(8):
                S.op("dve", lambda e, h=h: e.match_replace(out=cwork[:, h, :], in_to_replace=best[:, h, 0:8], in_values=cand[:, h, :], imm_value=NEG), ["cand", "ba%d" % h], ["cw%d" % h])
            for h in range(8):
                S.op("dve", lambda e, h=h: e.max(out=best[:, h, 8:16], in_=cwork[:, h, :]), ["cw%d" % h], ["best"])
            S.op("dve", lambda e: e.tensor_reduce(out=tau, in_=best, axis=AX.X, op=ALU.min), ["best"], ["tau"])
            tt("dve", mx, mm4[:, :, 0], mm4[:, :, 1], ALU.add, ["mm_"], ["mx"])
            tt("dve", ebest, best, bc(mx, 2, [128, 8, 16]), ALU.subtract, ["best", "mx"], ["ebest"])
            act(ebest, ebest, AF.Exp, ["ebest"], ["ebest"])
            tt("dve", E2, s4[:, :, 1, :], bc(mm4[:, :, 1], 2, [128, 8, 128]), ALU.subtract, [sk, "mm_"], ["E2"])
            act(E2b, E2, AF.Exp, ["E2"], ["E2b"])
            tt("dve", r1, v4[:, :, 0, :], bc(mm4[:, :, 0], 2, [128, 8, 16]), ALU.subtract, ["v_", "mm_"], ["r1"])
            act(r1, r1, AF.Exp, ["r1"], ["r1"])
            cw4 = cwork.rearrange("p h (a b) -> p h a b", a=16)
            tt("dve", cw4, cand.rearrange("p h (a b) -> p h a b", a=16), tau.unsqueeze(2).unsqueeze(3).to_broadcast([128, 8, 16, 16]), ALU.is_lt, ["cand", "tau"], ["cwork"])
            ts("dve", cw4, cw4, 1.0e9, ALU.mult, ["cwork"], ["cwork"])
            tt("dve", cw4, cw4, bc(v4[:, :, 1, :], 2, [128, 8, 16, 16]), ALU.max, ["cwork", "v_"], ["cwork"])
            S.op("dve", lambda e: e.tensor_reduce(out=thr, in_=cw4, axis=AX.X, op=ALU.min), ["cwork"], ["thr"])
            S.op("dve", lambda e: e.tensor_reduce(out=Z, in_=ebest, axis=AX.X, op=ALU.add), ["ebest"], ["Z"])
            S.op("dve", lambda e: e.reciprocal(out=rZ, in_=Z), ["Z"], ["rZ"])
            tt("dve", r1b, r1, bc(rZ, 2, [128, 8, 16]), ALU.mult, ["r1", "rZ"], ["r1b"])

        def passes(T):
            sk = "s_%d" % (T % 2)
            s4 = s2b[T % 2].rearrange("p (h q) n -> p h q n", q=2)
            tt("dve", tokAv, bc(s4[:, :, 1, :], 2, SH), bc(thr, 3, SH), ALU.is_ge, [sk, "thr"], ["tokA"])
            tt("dve", tokAv, tokAv, bc(E2b, 2, SH), ALU.mult, ["tokA", "E2b"], ["tokA"])
            tt("dve", tokBv, bc(s4[:, :, 0, :], 2, SH), bc(v4[:, :, 0, :], 3, SH), ALU.is_equal, [sk, "v_"], ["tokB"])
            tt("dve", tokBv, tokBv, bc(r1b, 3, SH), ALU.mult, ["tokB", "r1b"], ["tokB"])

        iec = {"n": 0}

        def transp(T):
            for src, sk_, dst, dk_ in ((tokA, "tokA", M2T, "M2T"), (tokB, "tokB", O1T, "O1T")):
                for i8 in range(16):
                    x = iec["n"] % 2; iec["n"] += 1
                    transposes([(ptrb[x][:, a, :], src[:, :, i8 * 8 + a]) for a in range(8)], ident_b, [sk_], ["ptrb%d" % x])
                    cp("act", dst[:, i8 * 8:(i8 + 1) * 8, :], ptrb[x], ["ptrb%d" % x], [dk_])

        def tokmm(T):
            for t8 in range(16):
                x = t8 % 2
                mmx([(pw[x][:, a, :], M2T[:, :, t8 * 8 + a], O1T[:, :, t8 * 8 + a], True, True) for a in range(8)], ["M2T", "O1T"], ["pw%d" % x])
                cp("act", Wt[:, :, t8 * 8:(t8 + 1) * 8], pw[x].rearrange("p a i -> p i a"), ["pw%d" % x], ["Wt"])
            for c8 in range(16):
                dma("sp", WD[T // 6, c8 * 8:(c8 + 1) * 8, :, (T % 6) * 128:(T % 6 + 1) * 128].rearrange("c p t -> p c t"), Wt[:, c8 * 8:(c8 + 1) * 8, :], ["Wt"], ["WD"])

        topk(0); passes(0)
        for T in range(NT):
            transp(T)
            if T + 1 < NT:
                topk(T + 1)
            tokmm(T)
            if T + 1 < NT:
                passes(T + 1)
        S.barrier()
        st["off"] = P_LATE
    if "wd" in dbg:
        dbg_out["d_wd"] = dout("d_wd", [128, 128, 128], BF16)
        dma("sp", dbg_out["d_wd"], WD[0, :, :, 0:128], [])

    if stop_after >= 6:
        G = 4
        h2 = sb([16, 768], BF16)
        acc = sb([6, D])
        P6_MARK = st["off"]
        NCH = 128
        NB = 3
        for hf in range(2):
            st["off"] = P6_MARK
            ust = [sb([16, 128]) for _ in range(NB)]; vst = [sb([D]) for _ in range(NB)]
            wt = [sb([6, 128], BF16) for _ in range(NB)]
            P7_MARK = st["off"]
            ub = [sb([16, 128], BF16) for _ in range(2)]; vb = [sb([G, D], BF16) for _ in range(2)]
            gl = [sb([768], BF16) for _ in range(2)]
            ST = [sb([G, 768], BF16) for _ in range(2)]
            hc0 = 0
            if hf == 0:
                load(h2, H2T[:, :, 0:768], "h2")
            pa = [[ps(0)[:, 0:384], ps(1)[:, 0:384]], [ps(2)[:, 0:384], ps(3)[:, 0:384]]]
            pb = [ps(4), ps(5), ps(6), ps(7)]

            def ld(c, hf_=hf):
                k = c % NB
                load(ust[k], UT_d[c].rearrange("p (k e) -> p k e", k=16), "ust%d" % k)
                load(vst[k], V_d[c * 128:(c + 1) * 128, :], "vst%d" % k)
                load(wt[k].rearrange("p t i -> p (t i)"), WD[hf_, c], "wt%d" % k, reads=["WD"])

            ib = {"n": 0}

            def phaseB(grp):
                g2 = grp % 2
                for tl in range(6):
                    for db in range(4):
                        x = ib["n"] % 4; ib["n"] += 1
                        mms(pb[x], [(ST[g2][:, ci, tl * 128:(tl + 1) * 128], vb[g2][:, ci, db * 512:(db + 1) * 512]) for ci in range(G)],
                            ["ST%d" % g2, "vb%d" % g2], ["pb%d" % x])
                        if grp == 0:
                            cp("dve", acc[:, tl, db * 512:(db + 1) * 512], pb[x], ["pb%d" % x], ["acc"])
                        else:
                            tt("dve", acc[:, tl, db * 512:(db + 1) * 512], acc[:, tl, db * 512:(db + 1) * 512], pb[x], ALU.add, ["pb%d" % x, "acc"], ["acc"])

            def casts(c):
                k = c % 2; k3 = c % NB; g2 = (c // G) % 2; ci = c % G
                cp("act", ub[k], ust[k3], ["ust%d" % k3], ["ub%d" % k])
                cp("act", vb[g2][:, ci, :], vst[k3], ["vst%d" % k3], ["vb%d" % g2])

            if hf == 0:
                ld(0); ld(1)
            casts(0)
            for c in range(NCH):
                k = c % 2; k3 = c % NB; grp = c // G; g2 = grp % 2; ci = c % G
                if c + 2 < NCH:
                    ld(c + 2)
                if c + 1 < NCH:
                    casts(c + 1)
                for blk in range(2):
                    mms(pa[k][blk], [(ub[k][:, dk, :], h2[:, dk, hc0 + blk * 384:hc0 + (blk + 1) * 384]) for dk in range(16)], ["ub%d" % k, "h2"], ["pa%d%d" % (k, blk)])
                    act(gl[k][:, blk * 384:(blk + 1) * 384], pa[k][blk], AF.Gelu, ["pa%d%d" % (k, blk)], ["gl%d" % k])
                tt("dve", ST[g2][:, ci, :], gl[k], wt[k3].rearrange("p t i -> p (t i)"), ALU.mult, ["gl%d" % k, "wt%d" % k3], ["ST%d" % g2])
                if ci == 0 and grp > 0:
                    phaseB(grp - 1)
            phaseB(NCH // G - 1)
            S.barrier()
            if hf == 0:
                load(h2, H2T[:, :, 768:1536], "h2")
                ld(0, 1); ld(1, 1)
            st["off"] = P7_MARK
            x1t = [sb([D]) for _ in range(2)]; g2bc = sb([D]); lng = sb([D]); lnb = sb([D]); yt = [sb([D]) for _ in range(2)]
            bst4 = sb([4, 6]); mv4 = sb([2]); rs4 = sb([1])
            load(lng, ln2g_d.partition_broadcast(128), "lng"); load(lnb, ln2b_d.partition_broadcast(128), "lnb")
            for tl in range(6):
                T = hf * 6 + tl; k = tl % 2
                c_ = 0 if T < 4 else 1
                if tl == 0 or T == 4:
                    load(g2bc, MODD[c_:c_ + 1, 5 * D:6 * D].partition_broadcast(128), "g2bc")
                load(x1t[k], X1[T], "x1t%d" % k)
                tt("dve", acc[:, tl, :], acc[:, tl, :], g2bc, ALU.mult, ["acc", "g2bc"], ["acc"])
                stt("dve", x1t[k], x1t[k], ALPHA, acc[:, tl, :], ALU.mult, ALU.add, ["x1t%d" % k, "acc"], ["x1t%d" % k])
                layer_norm(x1t[k], yt[k], lng, lnb, "x1t%d" % k, "yt%d" % k, bst4, mv4, rs4)
                store(y_d[T * 128:(T + 1) * 128, :], yt[k], "yt%d" % k)
            S.barrier()

    if "p1" in dbg:
        for nm, src, shp in (("d_QRT", QRT, [8, 128, NT * 128]), ("d_KR", KR, [NT + NO, 128, 1024]), ("d_QAT", QAT, [8, 128, NT * 128]),
                             ("d_KAT", KAT, [2, 128, (NT + NO) * 128]), ("d_VA", VA, [NT + NO, 128, 256]), ("d_GR", GR, [NT, 128, 1024])):
            dbg_out[nm] = dout(nm, shp, BF16)
            dma("sp", dbg_out[nm], src, [])

    S.barrier()
    S.emit()
    es.close()
    return dbg_out


def _rope_tables(pos):
    row = (pos // 64).astype(np.float32); col = (pos % 64).astype(np.float32)
    inv = (10000.0 ** (-np.arange(0, 64, 2, dtype=np.float32) / 64.0)).astype(np.float32)
    ar = row[:, None] * inv; ac = col[:, None] * inv
    ang = np.concatenate([ar, ar, ac, ac], -1).astype(np.float32)
    sgn = np.concatenate([-np.ones(32), np.ones(32), -np.ones(32), np.ones(32)]).astype(np.float32)
    return np.cos(ang).astype(np.float32), (np.sin(ang) * sgn).astype(np.float32)


def make_in_maps(inp):
    f = lambda a: np.ascontiguousarray(a, dtype=np.float32)
    UT = f(inp["peer_u"][0].reshape(128, 128, 16, 128).transpose(0, 3, 2, 1)).reshape(128, 128, D)
    keysT = f(np.transpose(inp["peer_sub_keys"][0], (0, 1, 3, 2)).reshape(16, 128, 128))
    shared = dict(
        w_ada=f(inp["w_ada"][0]), b_ada=f(inp["b_ada"]), w_in=f(inp["w_in"][0]), decay=f(inp["ret_decay_logit"][0].reshape(1, 16)),
        gn_g=f(inp["ret_gn_g"]), qn_g=f(inp["q_norm_g"]), kn_g=f(inp["k_norm_g"]), w_o=f(inp["w_o"][0]),
        ln1_g=f(inp["ln1_g"]), ln1_b=f(inp["ln1_b"]), w_q=f(inp["peer_w_q"][0]), keysT=keysT, UT=UT, V=f(inp["peer_v"][0]),
        ln2_g=f(inp["ln2_g"]), ln2_b=f(inp["ln2_b"]), ident=np.eye(128, dtype=np.float32),
        pert=np.tile(-4e-6 * np.arange(128, dtype=np.float32), 16).reshape(1, 2048),
    )
    i_ = np.arange(128)
    j_ = np.arange(128)
    dij = (i_[None, :] - j_[:, None]).astype(np.float32)
    shared["A1"] = np.maximum(dij, 0); shared["A2"] = np.maximum(-dij, 0)
    shared["U1"] = (dij >= 0).astype(np.float32); shared["L1"] = (dij <= 0).astype(np.float32)
    shared["zexp"] = np.stack([127.0 - j_, j_ * 1.0], 1).astype(np.float32)
    shared["xirow"] = np.stack([i_ + 1.0, 128.0 - i_], 0).astype(np.float32)
    maps = []
    for c in range(8):
        b, qd = c // 4, c % 4
        a, e = qd * 1024, (qd + 1) * 1024
        own = np.arange(a, e); oth = np.concatenate([np.arange(0, a), np.arange(e, 4096)])
        m = dict(shared)
        m["xown"] = f(np.concatenate([inp["x_prompt"][2 * c], inp["x_prompt"][2 * c + 1], inp["x_sample"][b, own]], 0))
        m["xoth"] = f(inp["x_sample"][b, oth])
        m["cond"] = f(np.stack([inp["c_ctx"], inp["c"][b]], 0))
        m["s0"] = f(inp["state_ret"][b, 0].reshape(16, 128, 128))
        m["ck"] = f(inp["cache_k"][b, 0].reshape(256, 256)); m["cv"] = f(inp["cache_v"][b, 0].reshape(256, 256))
        m["cos_own"], m["sin_own"] = _rope_tables(own)
        m["cos_oth"], m["sin_oth"] = _rope_tables(oth)
        ef = np.where(oth < a, a - 1 - oth, 0).astype(np.float32); mf = (oth < a).astype(np.float32)
        eb = np.where(oth >= e, oth - e, 0).astype(np.float32); mb = (oth >= e).astype(np.float32)
        m["oexp"] = f(np.stack([ef.reshape(NO, 128).T, eb.reshape(NO, 128).T], 1))
        m["omsk"] = f(np.stack([mf.reshape(NO, 128).T, mb.reshape(NO, 128).T], 1))
        m["s0c"] = np.array([[a, 4096 - e]], dtype=np.float32)
        maps.append(m)
    return maps


def kernel(**inp):
    inp = {k: np.asarray(v) for k, v in inp.items()}
    nc = bass.Bass("TRN2", target_bir_lowering=False)
    build(nc)
    maps = make_in_maps(inp)
    res = run_bass_kernel_spmd(nc, maps, core_ids=list(range(8)))
    R = res.results
    y_p = np.zeros((16, 256, D), np.float32); y_s = np.zeros((2, 4096, D), np.float32)
    nst = np.zeros((16, 1, 2, 8, 128, 128), np.float32); nk = np.zeros((16, 1, 256, 2, 128), np.float32); nv = np.zeros_like(nk)
    for c in range(8):
        b, qd = c // 4, c % 4
        y = R[c]["y"]
        y_p[2 * c] = y[0:256]; y_p[2 * c + 1] = y[256:512]
        y_s[b, qd * 1024:(qd + 1) * 1024] = y[512:]
        nst[2 * c:2 * c + 2, 0] = R[c]["nst"]
        nk[2 * c:2 * c + 2, 0] = R[c]["nk"].reshape(2, 256, 2, 128)
        nv[2 * c:2 * c + 2, 0] = R[c]["nv"].reshape(2, 256, 2, 128)
    return y_p, y_s, nst, nk, nv
```
